# Optimizing a Trainium2 kernel written in Bass

```python
import jax, jax.numpy as jnp
from jax import lax
import numpy as np

D_MODEL = 1024
BATCH = 8
SEQ = 2048
DEPTH = 2

MEM_LEN = 256
MEM_HEADS = 4
MEM_HEAD_DIM = D_MODEL // MEM_HEADS
D_FF = 2816
POOL_WINDOWS = (2, 4, 8, 16)
POOL_WIDTH = 512
POOL_GROUP = POOL_WIDTH // 4
NSA_HEADS = 8
NSA_KV_HEADS = 2
NSA_HPG = NSA_HEADS // NSA_KV_HEADS
NSA_HEAD_DIM = 64
NSA_WIDTH = NSA_HEADS * NSA_HEAD_DIM
NSA_KV_WIDTH = NSA_KV_HEADS * NSA_HEAD_DIM
CMP_BLOCK = 32
CMP_STRIDE = 16
CMP_HIDDEN = 256
SEL_BLOCK = 64
SEL_TOPK = 8
WINDOW = 256
Q_BLOCK = 128
GMLP_WIDTH = 512
GMLP_GROUPS = 4
GMLP_GROUP_DIM = GMLP_WIDTH // GMLP_GROUPS
GMLP_CHUNK = 128
IN_SIZES = (POOL_WIDTH, NSA_WIDTH, 6 * NSA_KV_WIDTH, 3 * NSA_HEADS, 2 * GMLP_WIDTH, 3 * D_MODEL)
IN_COLS = POOL_WIDTH + NSA_WIDTH + 6 * NSA_KV_WIDTH + 3 * NSA_HEADS + 2 * GMLP_WIDTH + 3 * D_MODEL
EPS = 1e-6
NEG = -1e30

kernel_name = "hybrid_pool_nsa_gmlp_macaron_block"


def rmsnorm(x, g):
    xf = x.astype(jnp.float32)
    y = xf * lax.rsqrt(jnp.mean(xf * xf, axis=-1, keepdims=True) + EPS)
    return (y * g.astype(jnp.float32)).astype(x.dtype)


def layernorm(x, g, b):
    xf = x.astype(jnp.float32)
    mu = jnp.mean(xf, axis=-1, keepdims=True)
    var = jnp.mean(jnp.square(xf - mu), axis=-1, keepdims=True)
    y = (xf - mu) * lax.rsqrt(var + EPS) * g.astype(jnp.float32) + b.astype(jnp.float32)
    return y.astype(x.dtype)


def masked_softmax(s, valid):
    s = jnp.where(valid, s.astype(jnp.float32), NEG)
    return jnp.where(valid, jax.nn.softmax(s, axis=-1), 0.0)


def swiglu(h, w1, w3, w2):
    return (jax.nn.silu(h @ w1) * (h @ w3)) @ w2


def pool_mixer(a, pool_w, pool_scale):
    B, S, _ = a.shape
    csum = jnp.cumsum(a.astype(jnp.float32), axis=1)
    count = jnp.arange(1, S + 1, dtype=jnp.float32)[:, None]
    means = []
    for gi, w in enumerate(POOL_WINDOWS):
        c = csum[..., gi * POOL_GROUP:(gi + 1) * POOL_GROUP]
        lagged = jnp.pad(c, ((0, 0), (w, 0), (0, 0)))[:, :S]
        means.append((c - lagged) / jnp.minimum(count, float(w)))
    pooled = jnp.concatenate(means, axis=-1).astype(a.dtype) - a
    pooled = pooled.reshape(B, S, len(POOL_WINDOWS), POOL_GROUP)
    mixed = jnp.einsum('bsgc,gcd->bsgd', pooled, pool_w).reshape(B, S, POOL_WIDTH)
    return mixed * pool_scale


def compress_blocks(kv, pos_emb, w1, w2):
    S = kv.shape[1]
    n_cmp = (S - CMP_BLOCK) // CMP_STRIDE + 1
    idx = np.arange(n_cmp)[:, None] * CMP_STRIDE + np.arange(CMP_BLOCK)[None, :]
    blocks = kv[:, idx] + pos_emb[:, None, :]
    h = jax.nn.gelu(jnp.einsum('bnlgd,ldh->bngh', blocks, w1))
    return jnp.einsum('bngh,hd->bngd', h, w2)


def cmp_sel_overlap(n_cmp, n_sel):
    cs = np.arange(n_cmp)[:, None] * CMP_STRIDE
    ss = np.arange(n_sel)[None, :] * SEL_BLOCK
    ov = np.minimum(cs + CMP_BLOCK, ss + SEL_BLOCK) - np.maximum(cs, ss)
    return (np.maximum(ov, 0) / CMP_BLOCK).astype(np.float32)


def nsa_mixer(q, k_cmp, v_cmp, k_slc, v_slc, k_win, v_win, gate_logits,
              cmp_pos_k, cmp_w1_k, cmp_w2_k, cmp_pos_v, cmp_w1_v, cmp_w2_v):
    B, S = q.shape[:2]
    G, HPG, dk = NSA_KV_HEADS, NSA_HPG, NSA_HEAD_DIM
    dt = q.dtype
    scale = dk ** -0.5
    qg = q.reshape(B, S, G, HPG, dk)
    pos = jnp.arange(S)

    n_cmp = (S - CMP_BLOCK) // CMP_STRIDE + 1
    kc = compress_blocks(k_cmp, cmp_pos_k, cmp_w1_k, cmp_w2_k)
    vc = compress_blocks(v_cmp, cmp_pos_v, cmp_w1_v, cmp_w2_v)
    s_cmp = jnp.einsum('bsghd,bngd->bghsn', qg, kc) * scale
    cmp_end = jnp.arange(n_cmp) * CMP_STRIDE + CMP_BLOCK - 1
    p_cmp = masked_softmax(s_cmp, cmp_end[None, :] <= pos[:, None])
    o_cmp = jnp.einsum('bghsn,bngd->bsghd', p_cmp.astype(dt), vc)

    n_sel = S // SEL_BLOCK
    top_k = min(SEL_TOPK, n_sel)
    imp = jnp.einsum('bghsn,nj->bgsj', p_cmp, jnp.asarray(cmp_sel_overlap(n_cmp, n_sel)))
    blk = jnp.arange(n_sel)[None, :]
    cur = (pos // SEL_BLOCK)[:, None]
    forced = (blk == 0) | (blk == cur) | (blk == cur - 1)
    valid = blk * SEL_BLOCK <= pos[:, None]
    imp = jnp.where(forced, 1e9, jnp.where(valid, imp, -1.0))
    _, sel_idx = lax.top_k(imp, top_k)

    n_q = S // Q_BLOCK
    ks_blk = k_slc.reshape(B, n_sel, SEL_BLOCK, G, dk).transpose(0, 3, 1, 2, 4)
    vs_blk = v_slc.reshape(B, n_sel, SEL_BLOCK, G, dk).transpose(0, 3, 1, 2, 4)
    bi = jnp.arange(B)[:, None, None, None]
    gi = jnp.arange(G)[None, :, None, None]
    q_blocks = qg.reshape(B, n_q, Q_BLOCK, G, HPG, dk).transpose(1, 0, 2, 3, 4, 5)
    idx_blocks = sel_idx.reshape(B, G, n_q, Q_BLOCK, top_k).transpose(2, 0, 1, 3, 4)
    starts = jnp.arange(n_q) * Q_BLOCK

    def sel_block(args):
        qb, ib, st = args
        k_sel = ks_blk[bi, gi, ib]
        v_sel = vs_blk[bi, gi, ib]
        s = jnp.einsum('btghd,bgtkld->bghtkl', qb, k_sel) * scale
        t = st + jnp.arange(Q_BLOCK)
        key_pos = ib[..., None] * SEL_BLOCK + jnp.arange(SEL_BLOCK)
        m = (key_pos <= t[None, None, :, None, None]).reshape(B, G, 1, Q_BLOCK, top_k * SEL_BLOCK)
        sh = s.shape
        p = masked_softmax(s.reshape(sh[0], sh[1], sh[2], sh[3], top_k * SEL_BLOCK), m).reshape(sh)
        return jnp.einsum('bghtkl,bgtkld->btghd', p.astype(v_sel.dtype), v_sel)

    o_sel = lax.map(sel_block, (q_blocks, idx_blocks, starts))
    o_sel = o_sel.transpose(1, 0, 2, 3, 4, 5).reshape(B, S, G, HPG, dk)

    n_shift = WINDOW // Q_BLOCK

    def windows(kv):
        kpad = jnp.pad(kv, ((0, 0), (WINDOW, 0), (0, 0), (0, 0)))
        kb = kpad.reshape(B, n_q + n_shift, Q_BLOCK, G, dk)
        return jnp.concatenate([kb[:, i:i + n_q] for i in range(n_shift + 1)], axis=2)

    kwb = windows(k_win)
    vwb = windows(v_win)
    qb_all = qg.reshape(B, n_q, Q_BLOCK, G, HPG, dk)
    s_win = jnp.einsum('bqtghd,bqjgd->bqghtj', qb_all, kwb) * scale
    i = jnp.arange(Q_BLOCK)[:, None]
    j = jnp.arange(WINDOW + Q_BLOCK)[None, :]
    key_pos = starts[:, None, None] - WINDOW + j[None]
    wmask = (j > i) & (j <= i + WINDOW) & (key_pos >= 0)
    p_win = masked_softmax(s_win, wmask[None, :, None, None])
    o_win = jnp.einsum('bqghtj,bqjgd->bqtghd', p_win.astype(dt), vwb).reshape(B, S, G, HPG, dk)

    g = jax.nn.sigmoid(gate_logits.reshape(B, S, G, HPG, 3))
    o = g[..., 0:1] * o_cmp + g[..., 1:2] * o_sel + g[..., 2:3] * o_win
    return o.reshape(B, S, NSA_WIDTH)


def gmlp_mixer(z, ln_g, ln_b, ws, bs):
    B, S, _ = z.shape
    z = jax.nn.gelu(z)
    u, v = jnp.split(z, 2, axis=-1)
    v = layernorm(v, ln_g, ln_b)
    n_chunk = S // GMLP_CHUNK
    v = v.reshape(B, n_chunk, GMLP_CHUNK, GMLP_GROUPS, GMLP_GROUP_DIM)
    causal = jnp.tril(jnp.ones((GMLP_CHUNK, GMLP_CHUNK), dtype=bool))
    w = jnp.where(causal[None], ws, 0.0).astype(v.dtype)
    s = jnp.einsum('gts,bcsgd->bctgd', w, v) + bs.T[None, None, :, :, None]
    return u * s.reshape(B, S, GMLP_WIDTH)


def token_mixer(h, w_in, b_in, pool_w, pool_scale,
                cmp_pos_k, cmp_w1_k, cmp_w2_k, cmp_pos_v, cmp_w1_v, cmp_w2_v,
                gmlp_ln_g, gmlp_ln_b, gmlp_ws, gmlp_bs,
                w_br_pool, w_br_nsa, w_br_gmlp, w_mix_out):
    B, S, _ = h.shape
    z = h @ w_in + b_in
    offs = np.cumsum(IN_SIZES)[:-1].tolist()
    a, q, kv, nsa_g, gm, mg = jnp.split(z, offs, axis=-1)
    q = q.reshape(B, S, NSA_HEADS, NSA_HEAD_DIM)
    kv = kv.reshape(B, S, 6, NSA_KV_HEADS, NSA_HEAD_DIM)
    y_pool = pool_mixer(a, pool_w, pool_scale) @ w_br_pool
    y_nsa = nsa_mixer(q, kv[:, :, 0], kv[:, :, 1], kv[:, :, 2], kv[:, :, 3], kv[:, :, 4], kv[:, :, 5], nsa_g,
                      cmp_pos_k, cmp_w1_k, cmp_w2_k, cmp_pos_v, cmp_w1_v, cmp_w2_v) @ w_br_nsa
    y_gmlp = gmlp_mixer(gm, gmlp_ln_g, gmlp_ln_b, gmlp_ws, gmlp_bs) @ w_br_gmlp
    g_pool, g_nsa, g_gmlp = jnp.split(jax.nn.sigmoid(mg), 3, axis=-1)
    return (g_pool * y_pool + g_nsa * y_nsa + g_gmlp * y_gmlp) @ w_mix_out


def memory_cross_attention(h, mem_n, wq, wk, wv, wo):
    B, S, _ = h.shape
    M = mem_n.shape[1]
    q = (h @ wq).reshape(B, S, MEM_HEADS, MEM_HEAD_DIM)
    k = (mem_n @ wk).reshape(B, M, MEM_HEADS, MEM_HEAD_DIM)
    v = (mem_n @ wv).reshape(B, M, MEM_HEADS, MEM_HEAD_DIM)
    s = jnp.einsum('bshd,bmhd->bhsm', q, k).astype(jnp.float32) * (MEM_HEAD_DIM ** -0.5)
    p = jax.nn.softmax(s, axis=-1).astype(v.dtype)
    o = jnp.einsum('bhsm,bmhd->bshd', p, v).reshape(B, S, D_MODEL)
    return o @ wo


def setup_inputs(seed: int = 0) -> dict:
    key = jax.random.key(seed)
    keys = iter(jax.random.split(key, 48))
    L = DEPTH

    def w(shape, fan_in):
        return jax.random.normal(next(keys), shape, jnp.float32) * fan_in ** -0.5

    def gain(shape):
        return 1.0 + 0.05 * jax.random.normal(next(keys), shape, jnp.float32)

    def small(shape):
        return 0.02 * jax.random.normal(next(keys), shape, jnp.float32)

    return {
        "x": jax.random.normal(next(keys), (BATCH, SEQ, D_MODEL), jnp.float32),
        "mem": jax.random.normal(next(keys), (BATCH, MEM_LEN, D_MODEL), jnp.float32),
        "ff1_pre_g": gain((L, D_MODEL)),
        "ff1_w1": w((L, D_MODEL, D_FF), D_MODEL),
        "ff1_w3": w((L, D_MODEL, D_FF), D_MODEL),
        "ff1_w2": w((L, D_FF, D_MODEL), D_FF),
        "ff1_post_g": gain((L, D_MODEL)),
        "mix_pre_g": gain((L, D_MODEL)),
        "w_in": w((L, D_MODEL, IN_COLS), D_MODEL),
        "b_in": small((L, IN_COLS)),
        "pool_w": w((L, len(POOL_WINDOWS), POOL_GROUP, POOL_GROUP), POOL_GROUP),
        "pool_scale": gain((L, POOL_WIDTH)),
        "cmp_pos_k": small((L, CMP_BLOCK, NSA_HEAD_DIM)),
        "cmp_w1_k": w((L, CMP_BLOCK, NSA_HEAD_DIM, CMP_HIDDEN), CMP_BLOCK * NSA_HEAD_DIM),
        "cmp_w2_k": w((L, CMP_HIDDEN, NSA_HEAD_DIM), CMP_HIDDEN),
        "cmp_pos_v": small((L, CMP_BLOCK, NSA_HEAD_DIM)),
        "cmp_w1_v": w((L, CMP_BLOCK, NSA_HEAD_DIM, CMP_HIDDEN), CMP_BLOCK * NSA_HEAD_DIM),
        "cmp_w2_v": w((L, CMP_HIDDEN, NSA_HEAD_DIM), CMP_HIDDEN),
        "gmlp_ln_g": gain((L, GMLP_WIDTH)),
        "gmlp_ln_b": small((L, GMLP_WIDTH)),
        "gmlp_ws": w((L, GMLP_GROUPS, GMLP_CHUNK, GMLP_CHUNK), GMLP_CHUNK),
        "gmlp_bs": gain((L, GMLP_GROUPS, GMLP_CHUNK)),
        "w_br_pool": w((L, POOL_WIDTH, D_MODEL), POOL_WIDTH),
        "w_br_nsa": w((L, NSA_WIDTH, D_MODEL), NSA_WIDTH),
        "w_br_gmlp": w((L, GMLP_WIDTH, D_MODEL), GMLP_WIDTH),
        "w_mix_out": w((L, D_MODEL, D_MODEL), D_MODEL),
        "mix_post_g": gain((L, D_MODEL)),
        "mem_pre_g": gain((L, D_MODEL)),
        "mem_kv_g": gain((L, D_MODEL)),
        "mem_wq": w((L, D_MODEL, D_MODEL), D_MODEL),
        "mem_wk": w((L, D_MODEL, D_MODEL), D_MODEL),
        "mem_wv": w((L, D_MODEL, D_MODEL), D_MODEL),
        "mem_wo": w((L, D_MODEL, D_MODEL), D_MODEL),
        "mem_post_g": gain((L, D_MODEL)),
        "ff2_pre_g": gain((L, D_MODEL)),
        "ff2_w1": w((L, D_MODEL, D_FF), D_MODEL),
        "ff2_w3": w((L, D_MODEL, D_FF), D_MODEL),
        "ff2_w2": w((L, D_FF, D_MODEL), D_FF),
        "ff2_post_g": gain((L, D_MODEL)),
    }


def reference(x, mem,
              ff1_pre_g, ff1_w1, ff1_w3, ff1_w2, ff1_post_g,
              mix_pre_g, w_in, b_in, pool_w, pool_scale,
              cmp_pos_k, cmp_w1_k, cmp_w2_k, cmp_pos_v, cmp_w1_v, cmp_w2_v,
              gmlp_ln_g, gmlp_ln_b, gmlp_ws, gmlp_bs,
              w_br_pool, w_br_nsa, w_br_gmlp, w_mix_out, mix_post_g,
              mem_pre_g, mem_kv_g, mem_wq, mem_wk, mem_wv, mem_wo, mem_post_g,
              ff2_pre_g, ff2_w1, ff2_w3, ff2_w2, ff2_post_g):
    for l in range(DEPTH):
        h = rmsnorm(x, ff1_pre_g[l])
        x = x + 0.5 * rmsnorm(swiglu(h, ff1_w1[l], ff1_w3[l], ff1_w2[l]), ff1_post_g[l])
        h = rmsnorm(x, mix_pre_g[l])
        y = token_mixer(h, w_in[l], b_in[l], pool_w[l], pool_scale[l],
                        cmp_pos_k[l], cmp_w1_k[l], cmp_w2_k[l], cmp_pos_v[l], cmp_w1_v[l], cmp_w2_v[l],
                        gmlp_ln_g[l], gmlp_ln_b[l], gmlp_ws[l], gmlp_bs[l],
                        w_br_pool[l], w_br_nsa[l], w_br_gmlp[l], w_mix_out[l])
        x = x + rmsnorm(y, mix_post_g[l])
        h = rmsnorm(x, mem_pre_g[l])
        y = memory_cross_attention(h, rmsnorm(mem, mem_kv_g[l]), mem_wq[l], mem_wk[l], mem_wv[l], mem_wo[l])
        x = x + rmsnorm(y, mem_post_g[l])
        h = rmsnorm(x, ff2_pre_g[l])
        x = x + 0.5 * rmsnorm(swiglu(h, ff2_w1[l], ff2_w3[l], ff2_w2[l]), ff2_post_g[l])
    return x
```

```python
import contextlib
import ml_dtypes
import numpy as np
import concourse.bass as bass
import concourse.mybir as mybir
from concourse.bass_utils import run_bass_kernel_spmd

F32 = mybir.dt.float32
BF16 = mybir.dt.bfloat16
AF = mybir.ActivationFunctionType
ALU = mybir.AluOpType

ENGS = ("pe", "act", "dve", "pool", "sp")
N_RING = 24
EPS = 1e-6
S = 2048
D = 1024
DFF = 2816
NT = 16
NEGM = -30000.0

CI_POOL, CI_Q, CI_KCMP, CI_VCMP, CI_KSLC, CI_KWIN, CI_U, CI_MG = 0, 4, 8, 9, 10, 12, 14, 18
N_FM = 42
TM_V, TM_G, TM_GV, N_TM = 0, 256, 280, 792


class Op:
    __slots__ = ("eng", "fn", "deps", "dma", "idx", "event", "signal", "ring_dep")

    def __init__(self, eng, fn, deps, dma, idx):
        self.eng, self.fn, self.deps, self.dma, self.idx = eng, fn, deps, dma, idx
        self.event = None
        self.signal = False
        self.ring_dep = None


class Prog:
    def __init__(self, nc):
        self.nc = nc
        self.ops = []
        self.last_w = {}
        self.readers = {}
        self.final_dma = []
        self.barrier_idx = None
        self.arena_keys = set()

    @staticmethod
    def _is_arena(k):
        return isinstance(k, tuple) and isinstance(k[0], str) and k[0].startswith("A:")

    def op(self, eng, fn, reads=(), writes=(), dma=False, final=False):
        idx = len(self.ops)
        deps = set()
        for r in list(reads) + list(writes):
            if r not in self.last_w and self._is_arena(r) and self.barrier_idx is not None:
                deps.add(self.barrier_idx)
            if self._is_arena(r):
                self.arena_keys.add(r)
        for r in reads:
            w = self.last_w.get(r)
            if w is not None:
                deps.add(w)
        for r in writes:
            w = self.last_w.get(r)
            if w is not None:
                deps.add(w)
            for rd in self.readers.get(r, ()):
                deps.add(rd)
        deps.discard(idx)
        o = Op(eng, fn, deps, dma, idx)
        self.ops.append(o)
        for r in reads:
            self.readers.setdefault(r, []).append(idx)
        for r in writes:
            self.last_w[r] = idx
            self.readers[r] = []
        if final:
            self.final_dma.append(idx)
        return idx

    def barrier(self, fn):
        keys = list(self.arena_keys)
        idx = self.op("dve", fn, reads=(), writes=keys)
        for k in keys:
            self.last_w.pop(k, None)
            self.readers.pop(k, None)
        self.arena_keys = set()
        self.barrier_idx = idx
        return idx

    def pe(self, fn, reads=(), writes=()):
        return self.op("pe", fn, reads, writes)

    def act(self, fn, reads=(), writes=()):
        return self.op("act", fn, reads, writes)

    def dve(self, fn, reads=(), writes=()):
        return self.op("dve", fn, reads, writes)

    def pool(self, fn, reads=(), writes=()):
        return self.op("pool", fn, reads, writes)

    def dma(self, eng, fn, reads=(), writes=(), final=False):
        return self.op(eng, fn, reads, writes, dma=True, final=final)

    def emit(self, stack):
        nc = self.nc
        ops = self.ops
        for o in ops:
            for d in o.deps:
                p = ops[d]
                if p.dma:
                    p.signal = True
                elif p.eng == "pe" and o.eng == "pe":
                    continue
                else:
                    p.signal = True
        for i in self.final_dma:
            ops[i].signal = True
        sem_eng = {e: stack.enter_context(nc.semaphore("s_" + e)) for e in ENGS}
        rings = {e: [stack.enter_context(nc.semaphore("r_%s%d" % (e, i))) for i in range(N_RING)]
                 for e in ("sp", "act", "pool")}
        cnt = {e: 0 for e in ENGS}
        dcnt = {e: 0 for e in rings}
        dma_hist = {e: [] for e in rings}
        for o in ops:
            if o.dma:
                k = dcnt[o.eng]
                dcnt[o.eng] += 1
                o.event = (rings[o.eng][k % N_RING], 16 * (k // N_RING + 1), 16)
                if k >= N_RING:
                    o.ring_dep = dma_hist[o.eng][k - N_RING]
                dma_hist[o.eng].append(o.event)
            elif o.signal:
                cnt[o.eng] += 1
                o.event = (sem_eng[o.eng], cnt[o.eng], 1)
        block = stack.enter_context(nc.Block())

        def run_engine(ename, eng):
            waited = {}
            for o in ops:
                if o.eng != ename:
                    continue
                evs = []
                for d in o.deps:
                    p = ops[d]
                    if p.event is None:
                        continue
                    if p.eng == "pe" and ename == "pe" and not p.dma:
                        continue
                    evs.append(p.event)
                if o.ring_dep is not None:
                    evs.append(o.ring_dep)
                need = {}
                for (s, v, _) in evs:
                    key = id(s)
                    if waited.get(key, 0) >= v:
                        continue
                    if key not in need or need[key][1] < v:
                        need[key] = (s, v)
                for key, (s, v) in need.items():
                    eng.wait_ge(s, v)
                    waited[key] = v
                ins = o.fn(eng)
                if o.event is not None:
                    ins.then_inc(o.event[0], o.event[2])
            if ename == "sp":
                for i in self.final_dma:
                    s, v, _ = ops[i].event
                    eng.wait_ge(s, v)

        @block.tensor
        def _(e):
            run_engine("pe", e)

        @block.scalar
        def _(e):
            run_engine("act", e)

        @block.vector
        def _(e):
            run_engine("dve", e)

        @block.gpsimd
        def _(e):
            run_engine("pool", e)

        @block.sync
        def _(e):
            run_engine("sp", e)


def lay_lhsT(W):
    K, N = W.shape
    return np.ascontiguousarray(W.reshape(K // 128, 128, N // 128, 128).transpose(2, 1, 0, 3))


def lay_rhs(W):
    K, N = W.shape
    return np.ascontiguousarray(W.reshape(K // 128, 128, N).transpose(1, 0, 2))


def lay_col(v):
    return np.ascontiguousarray(v.reshape(-1, 128).T)


def host_consts():
    c = {}
    c["ident"] = np.eye(128, dtype=np.float32)
    n = np.arange(127)[:, None]
    t = np.arange(S)[None, :]
    c["cmpmask"] = (16 * n + 31 <= t).astype(np.float32)
    cs = np.arange(127)[:, None] * 16
    ss = np.arange(32)[None, :] * 64
    ov = np.minimum(cs + 32, ss + 64) - np.maximum(cs, ss)
    c["ov"] = (np.maximum(ov, 0) / 32).astype(np.float32)
    c["eall"] = (np.arange(S)[None, :] // 64 == np.arange(32)[:, None]).astype(np.float32)
    p = np.arange(128)[:, None]
    q = np.arange(512)[None, :]
    caus = np.stack([np.where(q - (r * 128 + p) >= 0, 0.0, NEGM) for r in range(4)], 1)
    c["causneg"] = caus.astype(np.float32)
    win = np.stack([np.where((q - (r * 128 + p) >= 0) & (q - (r * 128 + p) < 256), 0.0, NEGM)
                    for r in range(-2, 4)], 1)
    c["winneg"] = win.astype(np.float32)
    pos = (np.arange(NT)[None, :] * 128 + np.arange(128)[:, None])
    cur = pos // 64
    j = np.arange(32)[None, None, :]
    forced = (j == 0) | (j == cur[:, :, None]) | (j == cur[:, :, None] - 1)
    valid = j * 64 <= pos[:, :, None]
    c["vmask"] = (valid & ~forced).astype(np.float32)
    c["addterm"] = np.where(forced, 1e9, np.where(valid, 0.0, -1.0)).astype(np.float32)
    fix = np.ones((128, 4, 16), np.float32)
    for gi, w in enumerate((2, 4, 8, 16)):
        tt = np.arange(16)
        fix[:, gi, :] = (w / np.minimum(tt + 1, w))[None, :]
    c["poolfix"] = fix
    c["trilT"] = (np.arange(128)[:, None] <= np.arange(128)[None, :]).astype(np.float32)
    for k in BF16_CONSTS:
        c[k] = c[k].astype(ml_dtypes.bfloat16)
    return c


BF16_CONSTS = ("cmpmask", "eall", "causneg", "winneg")
CONST_SHAPES = {"ident": [128, 128], "cmpmask": [127, S], "ov": [127, 32], "eall": [32, S],
                "causneg": [128, 4, 512], "winneg": [128, 6, 512], "vmask": [128, NT, 32],
                "addterm": [128, NT, 32], "poolfix": [128, 4, 16], "trilT": [128, 128]}


def layer_inputs(inp, l):
    d = {}
    for f in ("ff1", "ff2"):
        d[f + "_w1"] = lay_lhsT(inp[f + "_w1"][l])
        d[f + "_w3"] = lay_lhsT(inp[f + "_w3"][l])
        d[f + "_w2"] = lay_rhs(inp[f + "_w2"][l])
        d[f + "_pre"] = lay_col(inp[f + "_pre_g"][l])
        d[f + "_post"] = np.ascontiguousarray(inp[f + "_post_g"][l])
    W = inp["w_in"][l]
    b = inp["b_in"][l]
    dup = lambda a: np.concatenate([a, a], axis=-1)
    cols = [W[:, 0:512], W[:, 512:1024], W[:, 1024:1152], W[:, 1152:1280],
            dup(W[:, 1280:1344]), dup(W[:, 1344:1408]), dup(W[:, 1536:1600]), dup(W[:, 1600:1664]),
            W[:, 1816:2328], W[:, 2840:5912]]
    bcols = [b[0:512], b[512:1024], b[1024:1152], b[1152:1280],
             dup(b[1280:1344]), dup(b[1344:1408]), dup(b[1536:1600]), dup(b[1600:1664]),
             b[1816:2328], b[2840:5912]]
    d["win_fm"] = lay_lhsT(np.concatenate(cols, axis=1))
    d["bin_fm"] = lay_col(np.concatenate(bcols))
    tm = np.concatenate([W[:, 1408:1536], W[:, 1664:1792], W[:, 1792:1816], W[:, 2328:2840]], axis=1)
    d["win_tm"] = lay_rhs(tm)
    d["bin_tm"] = np.ascontiguousarray(np.concatenate([b[1408:1536], b[1664:1792], b[1792:1816], b[2328:2840]])[None, :])
    d["mix_pre"] = lay_col(inp["mix_pre_g"][l])
    d["mix_post"] = np.ascontiguousarray(inp["mix_post_g"][l])
    d["pool_w"] = np.ascontiguousarray(inp["pool_w"][l].transpose(1, 0, 2))
    d["pool_scale"] = lay_col(inp["pool_scale"][l])
    for kv in ("k", "v"):
        w1 = inp["cmp_w1_" + kv][l]
        w1p = w1.transpose(1, 0, 2)
        d["cw1_" + kv] = np.ascontiguousarray(np.concatenate([w1p, w1p], axis=0))
        pT = inp["cmp_pos_" + kv][l].T
        d["cpos_" + kv] = np.ascontiguousarray(np.concatenate([pT, pT], axis=0))
    w2k = inp["cmp_w2_k"][l]
    d["cw2_k"] = lay_rhs(dup(w2k))
    d["cw2_v"] = lay_rhs(inp["cmp_w2_v"][l])
    d["ln_g"] = np.ascontiguousarray(inp["gmlp_ln_g"][l])
    d["ln_b"] = np.ascontiguousarray(inp["gmlp_ln_b"][l])
    d["gws"] = np.ascontiguousarray(inp["gmlp_ws"][l].transpose(2, 0, 1))
    d["gbs"] = np.ascontiguousarray(inp["gmlp_bs"][l].reshape(-1))
    wbr = np.concatenate([inp["w_br_pool"][l], inp["w_br_nsa"][l], inp["w_br_gmlp"][l]], axis=1)
    d["w_br"] = lay_lhsT(wbr)
    d["w_mo"] = lay_rhs(inp["w_mix_out"][l])
    d["mem_pre"] = lay_col(inp["mem_pre_g"][l])
    d["mem_kvg"] = lay_col(inp["mem_kv_g"][l])
    d["mem_post"] = np.ascontiguousarray(inp["mem_post_g"][l])
    d["wq"] = lay_lhsT(inp["mem_wq"][l])
    d["wk"] = lay_lhsT(inp["mem_wk"][l])
    d["wv"] = lay_rhs(inp["mem_wv"][l])
    d["wo"] = lay_rhs(inp["mem_wo"][l])
    return d


class Ctx:
    pass


def build_program(layer_shapes, n_phase=None, depth=2):
    nc = bass.Bass("TRN2", target_bir_lowering=False)
    C = Ctx()
    C.nc = nc
    dram = {}
    dram["x"] = nc.dram_tensor("x", [S, D], F32, kind="ExternalInput").ap()
    dram["mem"] = nc.dram_tensor("mem", [256, D], F32, kind="ExternalInput").ap()
    for k, shp in CONST_SHAPES.items():
        dram["c_" + k] = nc.dram_tensor("c_" + k, shp, BF16 if k in BF16_CONSTS else F32, kind="ExternalInput").ap()
    for l in range(depth):
        for k, shp in layer_shapes.items():
            nm = "L%d_%s" % (l, k)
            dram[nm] = nc.dram_tensor(nm, list(shp), F32, kind="ExternalInput").ap()
    out = nc.dram_tensor("out", [S, D], F32, kind="ExternalOutput").ap()
    C.dram = dram

    with contextlib.ExitStack() as st:
        sb = lambda n, s, dt: st.enter_context(nc.sbuf_tensor(n, s, dt))
        P = Prog(nc)
        C.P = P
        C.X = sb("X", [128, NT, D], F32)
        ARENA_BYTES = 114 * 1024
        C.arena_bytes = ARENA_BYTES
        C.arena = sb("arena", [128, ARENA_BYTES // 2], BF16)
        C.PS = st.enter_context(nc.psum_tensor("PS", [128, 8, 512], F32))
        C.wring = sb("wring", [128, 5, 8, 128], BF16)
        C.wring_i = 0
        C.ident = sb("ident", [128, 128], BF16)
        C.zeros = sb("zeros", [128, 512], BF16)
        C.ones = sb("ones", [128, 128], BF16)
        C.junk = sb("junk", [128, D], BF16)
        C.hb = sb("hb", [128, 2, D], BF16)
        C.hb_i = 0
        C.ss = sb("ss", [128, 32], F32)
        C.rstd = sb("rstd", [128, 32], F32)
        C.gcol = sb("gcol", [128, 2, 8], F32)
        C.gcol_i = 0
        C.gpost = sb("gpost", [128, 1, D], F32)
        C.gpost_i = 0
        C.tmpf = sb("tmpf", [128, 1, D], F32)
        C.tmpf_i = 0
        C.ss2 = sb("ss2", [128, 4], F32)
        C.ss2_i = 0
        C.sg = sb("sg", [128, 2, 512], BF16)
        C.sg_i = 0
        C.ps_i = 0
        C.ps_lo, C.ps_hi = 0, 8
        C.small = sb("small", [128, 64], F32)
        C.epsb = sb("epsb", [128, 1], F32)
        C.bar = sb("bar", [128, 8], F32)
        P.dve(lambda e: e.memset(C.epsb[:], EPS), writes=["epsb"])

        P.dma("pool", lambda e: e.dma_start(out=C.ident[:], in_=dram["c_ident"]), writes=["ident"])
        P.dve(lambda e: e.memset(C.zeros[:], 0.0), writes=["zeros"])
        P.dve(lambda e: e.memset(C.ones[:], 1.0), writes=["ones"])
        C.e0 = sb("e0", [128, 128], BF16)
        P.dve(lambda e: e.memset(C.e0[:], 0.0), writes=["e0"])
        P.dve(lambda e: e.memset(C.e0[0:1, :], 1.0), writes=["e0"])
        xv = dram["x"].rearrange("(t p) d -> p t d", p=128)
        for t in range(NT):
            P.dma("sp", lambda e, t=t: e.dma_start(out=C.X[:, t, :], in_=xv[:, t, :]), writes=[("X", t)])

        phases = []
        for l in range(depth):
            phases += [("ffn", l, "ff1"), ("mixer", l, None), ("cross", l, None), ("ffn", l, "ff2")]
        if isinstance(n_phase, int):
            phases = phases[:n_phase]
        elif n_phase is not None:
            phases = list(n_phase)
        for (kind, l, which) in phases:
            if kind == "ffn":
                ffn_phase(C, l, which)
            elif kind == "mixer":
                mixer_phase(C, l)
            elif kind == "cross":
                cross_phase(C, l)

        ov = out.rearrange("(t p) d -> p t d", p=128)
        for t in range(NT):
            P.dma("sp", lambda e, t=t: e.dma_start(out=ov[:, t, :], in_=C.X[:, t, :]), reads=[("X", t)], final=True)
        P.emit(st)
    return nc


def carve(C, off, shape, dt):
    n = int(np.prod(shape[1:]))
    if dt == BF16:
        ap = C.arena[:, off // 2: off // 2 + n]
    else:
        ap = C.arena[:, off // 2: off // 2 + 2 * n].bitcast(F32)
    if len(shape) == 2:
        return ap
    names = "abcd"[: len(shape) - 1]
    pat = "p (%s) -> p %s" % (" ".join(names), " ".join(names))
    return ap.rearrange(pat, **{names[i]: shape[i + 1] for i in range(1, len(shape) - 1)})


def next_ps(C, n=1):
    i = C.ps_i
    if i < C.ps_lo or i >= C.ps_hi:
        i = C.ps_lo
    if n == 2:
        i = (i + 1) // 2 * 2
    if i + n > C.ps_hi:
        i = C.ps_lo
    C.ps_i = i + n
    return i


def psk(b, n=1):
    return [("ps", b + i) for i in range(n)]


def load_chunk(C, src_ap, tag, pkey=None, prefetch=False):
    if not hasattr(C, "prefetched"):
        C.prefetched = {}
    if pkey is not None and not prefetch and pkey in C.prefetched:
        return C.prefetched.pop(pkey)
    i = C.wring_i
    C.wring_i = (i + 1) % 5
    kc = src_ap.shape[1]
    dst = C.wring[:, i, 0:kc, :]
    C.P.dma("pool", lambda e: e.dma_start(out=dst, in_=src_ap), writes=[("wring", i)])
    if prefetch:
        C.prefetched[pkey] = (dst, ("wring", i))
    return dst, ("wring", i)


def load_gains(C, pre_ap, post_ap, post_scale):
    P = C.P
    gi = C.gcol_i
    C.gcol_i ^= 1
    pi = 0
    gcol = C.gcol[:, gi, :]
    gpost = C.gpost[:, pi, :]
    if pre_ap is not None:
        P.dma("sp", lambda e: e.dma_start(out=gcol, in_=pre_ap), writes=[("gcol", gi)])
    P.dma("sp", lambda e: e.dma_start(out=gpost, in_=post_ap.partition_broadcast(128)), writes=[("gpost", pi)])
    if post_scale != 1.0:
        P.dve(lambda e: e.tensor_scalar(out=gpost, in0=gpost, scalar1=post_scale, scalar2=None, op0=ALU.mult),
              reads=[("gpost", pi)], writes=[("gpost", pi)])
    return gcol, ("gcol", gi), gpost, ("gpost", pi)


def rms_stats(C, srcs, src_keys, col0):
    P = C.P
    n = len(srcs)
    for i, (s, k) in enumerate(zip(srcs, src_keys)):
        P.act(lambda e, s=s, i=i: e.activation(out=C.junk[:], in_=s, func=AF.Square, accum_out=C.ss[:, col0 + i: col0 + i + 1]),
              reads=[k], writes=["junk", ("ss", col0 + i)])
    cols = [("ss", col0 + i) for i in range(n)]
    rc = [("rstd", col0 + i) for i in range(n)]
    P.act(lambda e: e.activation(out=C.rstd[:, col0: col0 + n], in_=C.ss[:, col0: col0 + n], func=AF.Sqrt, scale=1.0 / D, bias=C.epsb[:, 0:1]),
          reads=cols + ["epsb"], writes=rc)
    P.dve(lambda e: e.reciprocal(out=C.rstd[:, col0: col0 + n], in_=C.rstd[:, col0: col0 + n]), reads=rc, writes=rc)


def norm_act(C, src, src_key, rcol):
    P = C.P
    hi = C.hb_i
    C.hb_i ^= 1
    hb = C.hb[:, hi, :]
    P.act(lambda e: e.activation(out=hb, in_=src, func=AF.Identity, scale=C.rstd[:, rcol: rcol + 1]),
          reads=[src_key, ("rstd", rcol)], writes=[("hb", hi)])
    return hi


def norm_pe(C, hi, gcol, gkey, dst, dst_key):
    P = C.P
    hb = C.hb[:, hi, :]
    b = next_ps(C)
    pv = C.PS[:, b, :].bitcast(BF16)
    for kc in range(8):
        P.pe(lambda e, kc=kc: e.transpose(out=pv[:, kc * 128:(kc + 1) * 128], in_=hb[:, kc * 128:(kc + 1) * 128], identity=C.ident[:]),
             reads=[("hb", hi), "ident"], writes=psk(b))
    P.dve(lambda e: e.tensor_tensor(out=dst, in0=pv.rearrange("p (k t) -> p k t", t=128),
                                    in1=gcol.unsqueeze(2).broadcast_to([128, 8, 128]), op=ALU.mult),
          reads=psk(b) + [gkey], writes=[dst_key])


def norm_transpose(C, src, src_key, rcol, gcol, gkey, dst, dst_key):
    hi = norm_act(C, src, src_key, rcol)
    norm_pe(C, hi, gcol, gkey, dst, dst_key)


def post_norm_update(C, b, t, gpost, gpkey, extra_reads=()):
    P = C.P
    y = C.PS[:, b:b + 2, :]
    si = C.ss2_i
    C.ss2_i = (si + 1) % 4
    ti = 0
    sc = C.ss2[:, si:si + 1]
    tmp = C.tmpf[:, ti, :]
    P.act(lambda e: e.activation(out=C.junk[:].rearrange("p (a b) -> p a b", b=512), in_=y, func=AF.Square, accum_out=sc),
          reads=psk(b, 2), writes=["junk", ("ss2", si)])
    P.act(lambda e: e.activation(out=sc, in_=sc, func=AF.Sqrt, scale=1.0 / D, bias=C.epsb[:, 0:1]),
          reads=[("ss2", si)], writes=[("ss2", si)])
    P.dve(lambda e: e.reciprocal(out=sc, in_=sc), reads=[("ss2", si)], writes=[("ss2", si)])
    P.dve(lambda e: e.scalar_tensor_tensor(out=tmp.rearrange("p (a b) -> p a b", b=512), in0=y, scalar=sc,
                                           in1=gpost.rearrange("p (a b) -> p a b", b=512), op0=ALU.mult, op1=ALU.mult),
          reads=psk(b, 2) + [("ss2", si), gpkey], writes=[("tmpf", ti)])
    P.dve(lambda e: e.tensor_tensor(out=C.X[:, t, :], in0=C.X[:, t, :], in1=tmp, op=ALU.add),
           reads=[("tmpf", ti), ("X", t)], writes=[("X", t)])


def ffn_phase(C, l, which):
    P, dram = C.P, C.dram
    C.ps_lo, C.ps_hi = 0, 8
    pre = "L%d_%s_" % (l, which)
    load_chunk(C, dram[pre + "w1"][0], "w1", pkey=(pre, "w1", 0), prefetch=True)
    load_chunk(C, dram[pre + "w3"][0], "w3", pkey=(pre, "w3", 0), prefetch=True)
    P.barrier(lambda e: e.memset(C.bar[:], 0.0))
    h1T = carve(C, 0, [128, 22, 1024], BF16)
    w2s = carve(C, 45056, [128, 22, 1024], BF16)
    hT = carve(C, 90112, [128, 8, 1024], BF16)
    gcol, gkey, gpost, gpkey = load_gains(C, dram[pre + "pre"], dram[pre + "post"], 0.5)
    w1d, w3d = dram[pre + "w1"], dram[pre + "w3"]
    rms_stats(C, [C.X[:, t, :] for t in range(8)], [("X", t) for t in range(8)], 0)
    for i in range(8):
        norm_transpose(C, C.X[:, i, :], ("X", i), i, gcol, gkey, hT[:, :, i * 128:(i + 1) * 128], ("A:hT", i))
    rms_stats(C, [C.X[:, t, :] for t in range(8, NT)], [("X", t) for t in range(8, NT)], 8)
    for grp in range(2):
        tiles = list(range(grp * 8, grp * 8 + 8))
        for c in range(22):
            a1, k1 = load_chunk(C, w1d[c], "w1", pkey=(pre, "w1", c) if grp == 0 else None)
            a3, k3 = load_chunk(C, w3d[c], "w3", pkey=(pre, "w3", c) if grp == 0 else None)
            if grp == 0 and c == 1:
                for hf in range(2):
                    P.dma("pool", lambda e, hf=hf: e.dma_start(out=w2s[:, hf * 11:(hf + 1) * 11, :], in_=dram[pre + "w2"][:, hf * 11:(hf + 1) * 11, :]),
                          writes=[("A:w2", hf)])
            for blk in range(2):
                ba = next_ps(C)
                bb = next_ps(C)
                hk = [("A:hT", blk * 4 + i) for i in range(4)]
                for kc in range(8):
                    P.pe(lambda e, kc=kc, ba=ba, a1=a1, blk=blk: e.matmul(C.PS[:, ba, :], lhsT=a1[:, kc, :], rhs=hT[:, kc, blk * 512:(blk + 1) * 512], start=(kc == 0), stop=(kc == 7)),
                         reads=[k1] + hk, writes=psk(ba))
                for kc in range(8):
                    P.pe(lambda e, kc=kc, bb=bb, a3=a3, blk=blk: e.matmul(C.PS[:, bb, :], lhsT=a3[:, kc, :], rhs=hT[:, kc, blk * 512:(blk + 1) * 512], start=(kc == 0), stop=(kc == 7)),
                         reads=[k3] + hk, writes=psk(bb))
                si = C.sg_i
                C.sg_i ^= 1
                sg = C.sg[:, si, :]
                P.act(lambda e, ba=ba, sg=sg: e.activation(out=sg, in_=C.PS[:, ba, :], func=AF.Silu), reads=psk(ba), writes=[("sg", si)])
                P.dve(lambda e, bb=bb, sg=sg, c=c, blk=blk: e.tensor_tensor(out=h1T[:, c, blk * 512:(blk + 1) * 512], in0=C.PS[:, bb, :], in1=sg, op=ALU.mult),
                      reads=psk(bb) + [("sg", si)], writes=[("A:h1T", c, blk)])
        nh = None
        if grp == 0:
            nh = norm_act(C, C.X[:, 8, :], ("X", 8), 8)
        for i, t in enumerate(tiles):
            if grp == 0:
                norm_pe(C, nh, gcol, gkey, hT[:, :, i * 128:(i + 1) * 128], ("A:hT", i))
                if i + 1 < 8:
                    nh = norm_act(C, C.X[:, 8 + i + 1, :], ("X", 8 + i + 1), 8 + i + 1)
            b = next_ps(C, 2)
            blk = i // 4
            for hf in range(2):
                for c in range(22):
                    P.pe(lambda e, c=c, hf=hf, b=b, i=i: e.matmul(C.PS[:, b + hf, :], lhsT=h1T[:, c, i * 128:(i + 1) * 128], rhs=w2s[:, c, hf * 512:(hf + 1) * 512], start=(c == 0), stop=(c == 21)),
                         reads=[("A:h1T", c, blk), ("A:w2", c // 11)], writes=psk(b + hf))
            post_norm_update(C, b, t, gpost, gpkey)


def fm_proj(C, src_chunk_ap, hT, hkeys, n, bias_ap, func, dst, dst_key, extra_reads=(), scale=None, psview=None, pkey=None):
    P = C.P
    a, k = load_chunk(C, src_chunk_ap, "fm", pkey=pkey)
    kcn = src_chunk_ap.shape[1]
    b = next_ps(C)
    for kc in range(kcn):
        P.pe(lambda e, kc=kc: e.matmul(C.PS[:, b, 0:n], lhsT=a[:, kc, :], rhs=hT[:, kc, 0:n], start=(kc == 0), stop=(kc == kcn - 1)),
             reads=[k] + list(hkeys), writes=psk(b))
    kw = {}
    if bias_ap is not None:
        kw["bias"] = bias_ap
    if scale is not None:
        kw["scale"] = scale
    src = C.PS[:, b, 0:n] if psview is None else psview(C.PS[:, b, 0:n])
    P.act(lambda e: e.activation(out=dst, in_=src, func=func, **kw), reads=psk(b) + [("A:binfm", 0)] + list(extra_reads), writes=[dst_key])


def mixer_phase(C, l):
    P, dram = C.P, C.dram
    pre = "L%d_" % l
    C.ps_lo, C.ps_hi = 0, 8
    load_chunk(C, dram[pre + "win_fm"][CI_KCMP], "fm", pkey=("m1", l, CI_KCMP), prefetch=True)
    load_chunk(C, dram[pre + "win_fm"][CI_VCMP], "fm", pkey=("m1", l, CI_VCMP), prefetch=True)
    P.barrier(lambda e: e.memset(C.bar[:], 0.0))
    off = [0]

    def A(shape, dt):
        ap = carve(C, off[0], shape, dt)
        nb = int(np.prod(shape[1:])) * (2 if dt == BF16 else 4)
        off[0] += (nb + 31) // 32 * 32
        return ap

    kslc = A([128, 2, S], BF16)
    kwin = A([128, 2, S], BF16)
    Vaug = A([128, NT, 4, 65], BF16)
    kc2 = A([128, 2, 128], BF16)
    Vc = A([128, 2, 98], BF16)
    binfm = A([128, N_FM], F32)
    btm = A([128, N_TM], BF16)
    hTp = A([128, 8, 512], BF16)
    halo = A([128, 8, 16], BF16)
    nsa_oT = A([128, 4, 512], BF16)
    wmo = A([128, 8, 1024], BF16)
    base = off[0]
    fm = dram[pre + "win_fm"]
    wtm = dram[pre + "win_tm"]
    P.dma("sp", lambda e: e.dma_start(out=binfm, in_=dram[pre + "bin_fm"]), writes=[("A:binfm", 0)])
    P.dve(lambda e: e.memset(btm, 0.0), writes=[("A:btm", 0)])
    P.dma("pool", lambda e: e.dma_start(out=btm[0:1, :], in_=dram[pre + "bin_tm"]), writes=[("A:btm", 0)])
    gcol, gkey, gpost, gpkey = load_gains(C, dram[pre + "mix_pre"], dram[pre + "mix_post"], 1.0)
    P.dve(lambda e: e.memset(Vaug[:, :, :, 64:65], 1.0), writes=[("A:Vones", 0)])
    P.dve(lambda e: e.memset(Vc[:, :, 64:65], 1.0), writes=[("A:Vcones", 0)])
    for g in range(2):
        P.dma("pool", lambda e, g=g: e.dma_start(out=Vc[0:127, g, 65:97], in_=dram["c_ov"]), writes=[("A:Vcov", g)])

    rms_stats(C, [C.X[:, t, :] for t in range(NT)], [("X", t) for t in range(NT)], 0)

    def do_norm(blk, par):
        tiles = list(range(blk * 4, blk * 4 + 4))
        hT = hTb[par]
        hkeys = [("A:hT", par, i) for i in range(4)]
        for i, t in enumerate(tiles):
            norm_transpose(C, C.X[:, t, :], ("X", t), blk * 4 + i, gcol, gkey, hT[:, :, i * 128:(i + 1) * 128], hkeys[i])
        return hT, hkeys

    off[0] = base
    hTb = [hTp, A([128, 8, 512], BF16)]
    kcmpT = A([128, S], BF16)
    vcmpT = A([128, S], BF16)
    wtv = A([128, 8, 256], BF16)
    cw1 = A([128, 32, 256], BF16)
    posT = A([128, 32], BF16)
    cw2k = A([128, 2, 128], BF16)
    cw2v = A([128, 2, 64], BF16)
    hcmp = A([128, 2, 2, 128], BF16)
    P.dma("pool", lambda e: e.dma_start(out=wtv, in_=wtm[:, :, 0:256]), writes=[("A:wtv", 0)])
    for blk in range(4):
        hT, hkeys = do_norm(blk, blk % 2)
        if blk == 1:
            for hf in range(2):
                P.dma("pool", lambda e, hf=hf: e.dma_start(out=wmo[:, hf * 4:(hf + 1) * 4, :], in_=dram[pre + "w_mo"][:, hf * 4:(hf + 1) * 4, :]), writes=[("A:wmo", hf)])
        cs = slice(blk * 512, (blk + 1) * 512)
        de = lambda T: T.rearrange("p (r q) -> p r q", q=128)[:, :, blk * 32:(blk + 1) * 32].rearrange("p r q -> p q r")
        pv3 = lambda ps: ps.rearrange("p (q r) -> p q r", r=16)
        fm_proj(C, fm[CI_KCMP], hT, hkeys, 512, binfm[:, CI_KCMP:CI_KCMP + 1], AF.Identity, de(kcmpT), ("A:kcmpT", blk), psview=pv3, pkey=("m1", l, CI_KCMP) if blk == 0 else None)
        fm_proj(C, fm[CI_VCMP], hT, hkeys, 512, binfm[:, CI_VCMP:CI_VCMP + 1], AF.Identity, de(vcmpT), ("A:vcmpT", blk), psview=pv3, pkey=("m1", l, CI_VCMP) if blk == 0 else None)
        for g in range(2):
            fm_proj(C, fm[CI_KSLC + g], hT, hkeys, 512, binfm[:, CI_KSLC + g:CI_KSLC + g + 1], AF.Identity, kslc[:, g, cs], ("A:kslc", g, blk))
            fm_proj(C, fm[CI_KWIN + g], hT, hkeys, 512, binfm[:, CI_KWIN + g:CI_KWIN + g + 1], AF.Identity, kwin[:, g, cs], ("A:kwin", g, blk))
        for i in range(4):
            t = blk * 4 + i
            b = next_ps(C)
            for kc in range(8):
                P.pe(lambda e, kc=kc, b=b, i=i, hT=hT: e.matmul(C.PS[:, b, 0:256], lhsT=hT[:, kc, i * 128:(i + 1) * 128], rhs=wtv[:, kc, :], start=(kc == 0), stop=False),
                     reads=[hkeys[i], ("A:wtv", 0)], writes=psk(b))
            P.pe(lambda e, b=b: e.matmul(C.PS[:, b, 0:256], lhsT=C.e0[:, :], rhs=btm[:, 0:256], start=False, stop=True),
                 reads=["e0", ("A:btm", 0)], writes=psk(b))
            P.dve(lambda e, b=b, t=t: e.tensor_copy(out=Vaug[:, t, :, 0:64], in_=C.PS[:, b, 0:256].rearrange("p (a d) -> p a d", d=64)),
                  reads=psk(b), writes=[("A:Vaug", t)])
    for ki, kv in enumerate(("k", "v")):
        srcT = kcmpT if kv == "k" else vcmpT
        skeys = [("A:%scmpT" % kv, blk) for blk in range(4)]
        for hf in range(2):
            P.dma("pool", lambda e, hf=hf, kv=kv: e.dma_start(out=cw1[:, hf * 16:(hf + 1) * 16, :], in_=dram[pre + "cw1_" + kv][:, hf * 16:(hf + 1) * 16, :]),
                  writes=[("A:cw1", hf)])
        P.dma("pool", lambda e, kv=kv: e.dma_start(out=posT, in_=dram[pre + "cpos_" + kv]), writes=[("A:posT", 0)])
        if kv == "k":
            P.dma("pool", lambda e: e.dma_start(out=cw2k, in_=dram[pre + "cw2_k"]), writes=[("A:cw2k", 0)])
        else:
            P.dma("pool", lambda e: e.dma_start(out=cw2v, in_=dram[pre + "cw2_v"]), writes=[("A:cw2v", 0)])
        for hc in range(2):
            b = next_ps(C)
            for lpos in range(32):
                P.pe(lambda e, lpos=lpos, b=b, hc=hc: e.matmul(C.PS[:, b, 0:1], lhsT=cw1[0:64, lpos, hc * 128:(hc + 1) * 128], rhs=posT[0:64, lpos:lpos + 1], start=(lpos == 0), stop=(lpos == 31)),
                     reads=[("A:cw1", lpos // 16), ("A:posT", 0)], writes=psk(b))
            P.dve(lambda e, b=b, hc=hc: e.tensor_copy(out=C.small[:, 8 + hc:9 + hc], in_=C.PS[:, b, 0:1]), reads=psk(b), writes=[("cb", hc)])
        for g in range(2):
            rows = slice(g * 64, g * 64 + 64)
            for hc in range(2):
                b = next_ps(C)
                for lpos in range(32):
                    P.pe(lambda e, lpos=lpos, b=b, hc=hc, rows=rows, srcT=srcT: e.matmul(C.PS[:, b, 0:127], lhsT=cw1[rows, lpos, hc * 128:(hc + 1) * 128], rhs=srcT[rows, (lpos % 16) * 128 + lpos // 16:(lpos % 16) * 128 + lpos // 16 + 127], start=(lpos == 0), stop=(lpos == 31)),
                         reads=[("A:cw1", lpos // 16)] + skeys, writes=psk(b))
                P.act(lambda e, b=b, hc=hc, g=g: e.activation(out=hcmp[:, hc, g, 0:127], in_=C.PS[:, b, 0:127], func=AF.Gelu_apprx_tanh, bias=C.small[:, 8 + hc:9 + hc]),
                      reads=psk(b) + [("cb", hc)], writes=[("A:hcmp", hc, g)])
            b = next_ps(C)
            if kv == "k":
                for hc in range(2):
                    P.pe(lambda e, b=b, hc=hc, g=g: e.matmul(C.PS[:, b, 0:127], lhsT=cw2k[:, hc, :], rhs=hcmp[:, hc, g, 0:127], start=(hc == 0), stop=(hc == 1)),
                         reads=[("A:cw2k", 0), ("A:hcmp", hc, g)], writes=psk(b))
                P.dve(lambda e, b=b, g=g: e.tensor_copy(out=kc2[:, g, 0:127], in_=C.PS[:, b, 0:127]), reads=psk(b), writes=[("A:kc2", g)])
            else:
                for hc in range(2):
                    P.pe(lambda e, b=b, hc=hc, g=g: e.matmul(C.PS[0:127, b, 0:64], lhsT=hcmp[:, hc, g, 0:127], rhs=cw2v[:, hc, :], start=(hc == 0), stop=(hc == 1)),
                         reads=[("A:cw2v", 0), ("A:hcmp", hc, g)], writes=psk(b))
                P.dve(lambda e, b=b, g=g: e.tensor_copy(out=Vc[0:127, g, 0:64], in_=C.PS[0:127, b, 0:64]), reads=psk(b), writes=[("A:Vc", g)])

    for blk in range(4):
        cs = slice(blk * 512, (blk + 1) * 512)
        for c in range(2):
            load_chunk(C, fm[CI_Q + c], "q", pkey=("q", l, blk, c), prefetch=True)
        P.barrier(lambda e: e.memset(C.bar[:], 0.0))
        C.ps_lo, C.ps_hi = 0, 4
        off[0] = base
        cmpmask = A([128, 512], BF16)
        eall = A([128, S], BF16)
        causneg = A([128, 4, 512], BF16)
        winneg = A([128, 6, 512], BF16)
        qP = A([128, 8, 512], BF16)
        Er = [A([128, 512], BF16) for _ in range(4)]
        oacc = A([128, 4, 512], F32)
        negmT = A([128, 2, 512], BF16)
        gates = A([128, 4, 24], F32)
        wg = A([128, 8, 24], BF16)
        vmask = A([128, 4, 32], F32)
        addt = A([128, 4, 32], F32)
        imp = A([128, 4, 32], F32)
        prod = A([128, 4, 4, 32], F32)
        m8 = A([128, 4, 8], F32)
        negm = A([128, 4, 128], BF16)
        fac = A([128, 16], F32)
        rinv = A([128, 16], F32)
        tmpo = A([128, 2, 256], F32)
        otok = A([128, 512], BF16)
        accsb = A([128, 4, 388], F32)
        if blk > 0:
            P.act(lambda e: e.copy(out=halo, in_=hTp[:, :, 496:512]), reads=[("A:hT", 0, 3)], writes=[("A:halo", 0)])
        hT, hkeys = do_norm(blk, 0)
        cs = slice(blk * 512, (blk + 1) * 512)
        P.dma("sp", lambda e, cmpmask=cmpmask, cs=cs: e.dma_start(out=cmpmask[0:127, :], in_=dram["c_cmpmask"][:, cs]), writes=[("A:cmpmask", 0)])
        P.dve(lambda e, eall=eall: e.memset(eall, 0.0), writes=[("A:eall", 0)])
        P.dma("sp", lambda e, eall=eall: e.dma_start(out=eall[0:32, :], in_=dram["c_eall"]), writes=[("A:eall", 0)])
        P.dma("sp", lambda e, causneg=causneg: e.dma_start(out=causneg, in_=dram["c_causneg"]), writes=[("A:causneg", 0)])
        P.dma("sp", lambda e, winneg=winneg: e.dma_start(out=winneg, in_=dram["c_winneg"]), writes=[("A:winneg", 0)])
        P.dve(lambda e, qP=qP: e.memset(qP, 0.0), writes=[("A:qP", h) for h in range(8)])
        P.dve(lambda e, negmT=negmT: e.memset(negmT, 0.0), writes=[("A:negmT", g) for g in range(2)])
        P.dve(lambda e, negm=negm: e.memset(negm, 0.0), writes=[("A:negm", 0)])
        P.dma("pool", lambda e, wg=wg: e.dma_start(out=wg, in_=wtm[:, :, TM_G:TM_G + 24]), writes=[("A:wg", 0)])
        P.dma("sp", lambda e, vmask=vmask, blk=blk: e.dma_start(out=vmask, in_=dram["c_vmask"][:, blk * 4:(blk + 1) * 4, :]), writes=[("A:vmask", 0)])
        P.dma("sp", lambda e, addt=addt, blk=blk: e.dma_start(out=addt, in_=dram["c_addterm"][:, blk * 4:(blk + 1) * 4, :]), writes=[("A:addt", 0)])
        for c in range(4):
            a, k = load_chunk(C, fm[CI_Q + c], "q", pkey=("q", l, blk, c))
            b = next_ps(C)
            for kc in range(8):
                P.pe(lambda e, kc=kc, b=b, a=a, hT=hT: e.matmul(C.PS[:, b, :], lhsT=a[:, kc, :], rhs=hT[:, kc, :], start=(kc == 0), stop=(kc == 7)),
                     reads=[k] + hkeys, writes=psk(b))
            for half in range(2):
                rs = slice(half * 64, half * 64 + 64)
                P.act(lambda e, b=b, rs=rs, c=c, half=half, qP=qP: e.activation(out=qP[rs, 2 * c + half, :], in_=C.PS[rs, b, :], func=AF.Identity, bias=binfm[rs, CI_Q + c:CI_Q + c + 1]),
                      reads=psk(b) + [("A:binfm", 0)], writes=[("A:qP", 2 * c + half)])
        for i in range(4):
            b = next_ps(C)
            for kc in range(8):
                P.pe(lambda e, kc=kc, b=b, i=i, hT=hT, wg=wg: e.matmul(C.PS[:, b, 0:24], lhsT=hT[:, kc, i * 128:(i + 1) * 128], rhs=wg[:, kc, :], start=(kc == 0), stop=False),
                     reads=[hkeys[i], ("A:wg", 0)], writes=psk(b))
            P.pe(lambda e, b=b: e.matmul(C.PS[:, b, 0:24], lhsT=C.e0[:, :], rhs=btm[:, TM_G:TM_G + 24], start=False, stop=True),
                 reads=["e0", ("A:btm", 0)], writes=psk(b))
            P.act(lambda e, b=b, i=i, gates=gates: e.activation(out=gates[:, i, :], in_=C.PS[:, b, 0:24], func=AF.Sigmoid), reads=psk(b), writes=[("A:gates", i)])
        gkeys = [("A:gates", i) for i in range(4)]
        e_i = [0]
        groups = [(0, 0), (0, 1), (2, 0), (2, 1), (1, 0), (1, 1)]
        seq = []
        for gidx, (br, g) in enumerate(groups):
            if br == 0:
                kts = [0]
            elif br == 1:
                kts = list(range(0, 4 * blk + 4))
            else:
                kts = list(range(max(0, 4 * blk - 2), 4 * blk + 4))
            tl = [(gidx, hh, kt) for hh in range(4) for kt in kts]
            for n_, t_ in enumerate(tl):
                seq.append(t_ + (n_ == 0, n_ == len(tl) - 1))
        tinfo = {}

        def front(tile, blk=blk, cs=cs, hT=hT):
            gidx, hh, kt, first, last = tile
            br, g = groups[gidx]
            h = 4 * g + hh
            bS = next_ps(C)
            ks = slice(kt * 128, (kt + 1) * 128)
            if br == 0:
                P.pe(lambda e: e.matmul(C.PS[0:127, bS, :], lhsT=kc2[:, g, 0:127], rhs=qP[:, h, :], start=True, stop=True),
                     reads=[("A:qP", h)], writes=psk(bS))
            elif br == 1:
                diag = kt >= 4 * blk
                P.pe(lambda e: e.matmul(C.PS[:, bS, :], lhsT=kslc[:, g, ks], rhs=qP[:, h, :], start=True, stop=False),
                     reads=[("A:qP", h)], writes=psk(bS))
                P.pe(lambda e: e.matmul(C.PS[:, bS, :], lhsT=eall[:, ks], rhs=negmT[:, g, :], start=False, stop=(not diag)),
                     reads=[("A:negmT", g), ("A:eall", 0)], writes=psk(bS))
                if diag:
                    P.pe(lambda e: e.matmul(C.PS[:, bS, :], lhsT=C.ident[:], rhs=causneg[:, kt - 4 * blk, :], start=False, stop=True),
                         reads=["ident", ("A:causneg", 0)], writes=psk(bS))
            else:
                P.pe(lambda e: e.matmul(C.PS[:, bS, :], lhsT=kwin[:, g, ks], rhs=qP[:, h, :], start=True, stop=False),
                     reads=[("A:qP", h)], writes=psk(bS))
                P.pe(lambda e: e.matmul(C.PS[:, bS, :], lhsT=C.ident[:], rhs=winneg[:, kt - 4 * blk + 2, :], start=False, stop=True),
                     reads=["ident", ("A:winneg", 0)], writes=psk(bS))
            si = e_i[0]
            e_i[0] = (si + 1) % 4
            Et = Er[si]
            np_ = 127 if br == 0 else 128
            P.act(lambda e: e.activation(out=Et[0:np_, :], in_=C.PS[0:np_, bS, :], func=AF.Exp, scale=0.125),
                  reads=psk(bS), writes=[("A:E", si)])
            if br == 0:
                P.dve(lambda e: e.tensor_tensor(out=Et[0:127, :], in0=Et[0:127, :], in1=cmpmask[0:127, :], op=ALU.mult),
                      reads=[("A:E", si), ("A:cmpmask", 0)], writes=[("A:E", si)])
            tinfo[tile] = (si, Et, np_)

        def back(tile, blk=blk, cs=cs):
            gidx, hh, kt, first, last = tile
            br, g = groups[gidx]
            W = 97 if br == 0 else 65
            si, Et, np_ = tinfo[tile]
            if first:
                for i in range(4):
                    P.pe(lambda e, i=i: e.matmul(C.PS[:, 4 + i, :], lhsT=C.zeros[:, 0:128], rhs=C.zeros[:, :], start=True, stop=True),
                         reads=["zeros"], writes=psk(4 + i))
            for i in range(4):
                qt = 4 * blk + i
                if br == 1 and kt > qt:
                    continue
                if br == 2 and not (qt - 2 <= kt <= qt):
                    continue
                if br == 0:
                    rhs = Vc[0:127, g, 0:97]
                elif br == 1:
                    rhs = Vaug[:, kt, g, :]
                else:
                    rhs = Vaug[:, kt, 2 + g, :]
                P.pe(lambda e, i=i, rhs=rhs: e.matmul(C.PS[:, 4 + i, hh * W:(hh + 1) * W], lhsT=Et[0:np_, i * 128:(i + 1) * 128], rhs=rhs, start=False, stop=False, skip_group_check=True),
                     reads=[("A:E", si)], writes=psk(4 + i))
            if last:
                post(br, g, W)

        def post(br, g, W):
            P.dve(lambda e: e.tensor_copy(out=accsb[:, :, 0:4 * W], in_=C.PS[:, 4:8, 0:4 * W]), reads=psk(4, 4), writes=[("A:accsb", 0)])
            accs = accsb[:, :, 0:4 * W].rearrange("p t (h w) -> p t h w", w=W)
            ak = [("A:accsb", 0)]
            r4 = rinv.rearrange("p (t h o) -> p t h o", h=4, o=1)
            P.dve(lambda e: e.tensor_scalar(out=r4, in0=accs[:, :, :, 64:65], scalar1=1e-30, scalar2=None, op0=ALU.max),
                  reads=ak, writes=[("A:rinv", 0)])
            P.dve(lambda e: e.reciprocal(out=rinv, in_=rinv), reads=[("A:rinv", 0)], writes=[("A:rinv", 0)])
            if br == 0:
                P.dve(lambda e: e.tensor_tensor(out=prod, in0=accs[:, :, :, 65:97], in1=r4.broadcast_to([128, 4, 4, 32]), op=ALU.mult),
                      reads=ak + [("A:rinv", 0)], writes=[("A:prod", 0)])
                P.dve(lambda e: e.tensor_tensor(out=imp, in0=prod[:, :, 0, :], in1=prod[:, :, 1, :], op=ALU.add), reads=[("A:prod", 0)], writes=[("A:imp", 0)])
                P.dve(lambda e: e.tensor_tensor(out=imp, in0=imp, in1=prod[:, :, 2, :], op=ALU.add), reads=[("A:prod", 0), ("A:imp", 0)], writes=[("A:imp", 0)])
                P.dve(lambda e: e.tensor_tensor(out=imp, in0=imp, in1=prod[:, :, 3, :], op=ALU.add), reads=[("A:prod", 0), ("A:imp", 0)], writes=[("A:imp", 0)])
                P.dve(lambda e: e.tensor_tensor(out=imp, in0=imp, in1=vmask, op=ALU.mult), reads=[("A:imp", 0), ("A:vmask", 0)], writes=[("A:imp", 0)])
                P.dve(lambda e: e.tensor_tensor(out=imp, in0=imp, in1=addt, op=ALU.add), reads=[("A:imp", 0), ("A:addt", 0)], writes=[("A:imp", 0)])
                for i in range(4):
                    P.dve(lambda e, i=i: e.max(out=m8[:, i, :], in_=imp[:, i, :]), reads=[("A:imp", 0)], writes=[("A:m8", i)])
                P.dve(lambda e: e.tensor_tensor(out=imp, in0=imp, in1=m8[:, :, 7:8].broadcast_to([128, 4, 32]), op=ALU.is_ge),
                      reads=[("A:imp", 0)] + [("A:m8", i) for i in range(4)], writes=[("A:imp", 0)])
                P.dve(lambda e: e.tensor_scalar(out=negm[:, :, 0:32], in0=imp, scalar1=-NEGM, scalar2=NEGM, op0=ALU.mult, op1=ALU.add),
                      reads=[("A:imp", 0)], writes=[("A:negm", 0)])
                bT = next_ps(C)
                pv = C.PS[:, bT, :].bitcast(BF16)
                for i in range(4):
                    P.pe(lambda e, i=i: e.transpose(out=pv[:, i * 128:(i + 1) * 128], in_=negm[:, i, :], identity=C.ident[:]), reads=[("A:negm", 0), "ident"], writes=psk(bT))
                P.act(lambda e: e.copy(out=negmT[0:32, g, :], in_=pv[0:32, 0:512]), reads=psk(bT), writes=[("A:negmT", g)])
            gv = gates.rearrange("p t (h r) -> p t h r", r=3)[:, :, 4 * g:4 * g + 4, br:br + 1]
            f4 = fac.rearrange("p (t h o) -> p t h o", h=4, o=1)
            P.dve(lambda e: e.tensor_tensor(out=f4, in0=r4, in1=gv, op=ALU.mult), reads=[("A:rinv", 0)] + gkeys, writes=[("A:fac", 0)])
            for hp in range(2):
                ts = slice(hp * 2, hp * 2 + 2)
                od = oacc[:, ts, g * 256:(g + 1) * 256].rearrange("p t (h d) -> p t h d", d=64)
                fb = f4[:, ts].broadcast_to([128, 2, 4, 64])
                okeys = [("A:oacc", hp, g)]
                if br == 0:
                    P.dve(lambda e, od=od, fb=fb, ts=ts: e.tensor_tensor(out=od, in0=accs[:, ts, :, 0:64], in1=fb, op=ALU.mult), reads=ak + [("A:fac", 0)], writes=okeys)
                else:
                    tv = tmpo[:, hp, :].rearrange("p (t h d) -> p t h d", t=2, d=64) if False else tmpo.rearrange("p t (h d) -> p t h d", d=64)
                    P.dve(lambda e, tv=tv, fb=fb, ts=ts: e.tensor_tensor(out=tv, in0=accs[:, ts, :, 0:64], in1=fb, op=ALU.mult), reads=ak + [("A:fac", 0)], writes=[("A:tmpo", 0)])
                    P.dve(lambda e, od=od, tv=tv: e.tensor_tensor(out=od, in0=od, in1=tv, op=ALU.add), reads=[("A:tmpo", 0)] + okeys, writes=okeys)

        SKEW = 2
        for n_ in range(len(seq) + SKEW):
            if n_ < len(seq):
                front(seq[n_])
            if n_ >= SKEW:
                back(seq[n_ - SKEW])
        for i in range(4):
            P.act(lambda e, i=i, otok=otok, oacc=oacc: e.copy(out=otok, in_=oacc[:, i, :]), reads=[("A:oacc", i // 2, 0), ("A:oacc", i // 2, 1)], writes=[("A:otok", 0)])
            bT = next_ps(C)
            pv = C.PS[:, bT, :].bitcast(BF16)
            for c in range(4):
                P.pe(lambda e, pv=pv, c=c, otok=otok: e.transpose(out=pv[:, c * 128:(c + 1) * 128], in_=otok[:, c * 128:(c + 1) * 128], identity=C.ident[:]),
                     reads=[("A:otok", 0), "ident"], writes=psk(bT))
            P.dve(lambda e, pv=pv, i=i: e.tensor_copy(out=nsa_oT[:, :, i * 128:(i + 1) * 128], in_=pv[:, 0:512].rearrange("p (c t) -> p c t", t=128)),
                  reads=psk(bT), writes=[("A:nsa_oT", i)])

        load_chunk(C, fm[CI_POOL], "pool", pkey=("pool", l, blk, 0), prefetch=True)
        load_chunk(C, fm[CI_U], "u", pkey=("u", l, blk, 0), prefetch=True)
        load_chunk(C, fm[CI_U + 1], "u", pkey=("u", l, blk, 1), prefetch=True)
        P.barrier(lambda e: e.memset(C.bar[:], 0.0))
        C.ps_lo, C.ps_hi = 0, 8
        off[0] = base
        abuf = [A([128, 528], F32) for _ in range(3)]
        pooled = A([128, 512], BF16)
        poolmix = A([128, 4, 512], BF16)
        pw = A([128, 4, 128], BF16)
        pscale = A([128, 4], F32)
        pfix = A([128, 4, 16], F32)
        uT = A([128, 4, 512], BF16)
        gvf = [A([128, 512], F32) for _ in range(2)]
        vn = [A([128, 512], BF16) for _ in range(2)]
        gws = A([128, 4, 128], BF16)
        tril = A([128, 128], BF16)
        lng = A([128, 512], F32)
        lnb = A([128, 512], F32)
        bsb = A([128, 512], F32)
        wgv = A([128, 8, 512], BF16)
        merged = A([128, 8, 512], BF16)
        gsig = [[A([128, 512], BF16) for _ in range(3)] for _ in range(2)]
        tA = [A([128, 512], F32) for _ in range(2)]
        bst = A([128, 2, 8], F32)
        assert off[0] <= C.arena_bytes, off[0]
        C.m2b_used = off[0]
        hT, hkeys = hTp, [("A:hT", 0, i) for i in range(4)]
        nsak = [("A:nsa_oT", i) for i in range(4)]
        P.dma("pool", lambda e, pw=pw: e.dma_start(out=pw, in_=dram[pre + "pool_w"]), writes=[("A:pw", 0)])
        P.dma("sp", lambda e, pscale=pscale: e.dma_start(out=pscale, in_=dram[pre + "pool_scale"]), writes=[("A:pscale", 0)])
        P.dma("sp", lambda e, pfix=pfix: e.dma_start(out=pfix, in_=dram["c_poolfix"]), writes=[("A:pfix", 0)])
        P.dma("sp", lambda e, lng=lng: e.dma_start(out=lng, in_=dram[pre + "ln_g"].partition_broadcast(128)), writes=[("A:lng", 0)])
        P.dma("sp", lambda e, lnb=lnb: e.dma_start(out=lnb, in_=dram[pre + "ln_b"].partition_broadcast(128)), writes=[("A:lnb", 0)])
        P.dma("sp", lambda e, bsb=bsb: e.dma_start(out=bsb, in_=dram[pre + "gbs"].partition_broadcast(128)), writes=[("A:bsb", 0)])

        def late_loads(gws=gws, tril=tril, wgv=wgv):
            P.dma("pool", lambda e: e.dma_start(out=gws, in_=dram[pre + "gws"]), writes=[("A:gws", 0)])
            P.dma("pool", lambda e: e.dma_start(out=tril, in_=dram["c_trilT"]), writes=[("A:tril", 0)])
            P.dve(lambda e: e.tensor_tensor(out=gws, in0=gws, in1=tril.unsqueeze(1).broadcast_to([128, 4, 128]), op=ALU.mult),
                  reads=[("A:gws", 0), ("A:tril", 0)], writes=[("A:gws", 0)])
            P.dma("pool", lambda e: e.dma_start(out=wgv, in_=wtm[:, :, TM_GV:TM_GV + 512]), writes=[("A:wgv", 0)])

        A0, A1, A2 = abuf

        def pool_front(gi, blk=blk, hT=hT, hkeys=hkeys):
            a, k = load_chunk(C, fm[CI_POOL + gi], "pool", pkey=("pool", l, blk, gi))
            b = next_ps(C)
            for kc in range(8):
                P.pe(lambda e, kc=kc: e.matmul(C.PS[:, b, :], lhsT=a[:, kc, :], rhs=hT[:, kc, :], start=(kc == 0), stop=(kc == 7)),
                     reads=[k] + hkeys, writes=psk(b))
            bcol = binfm[:, CI_POOL + gi:CI_POOL + gi + 1]
            if blk > 0:
                b2 = next_ps(C)
                for kc in range(8):
                    P.pe(lambda e, kc=kc: e.matmul(C.PS[:, b2, 0:16], lhsT=a[:, kc, :], rhs=halo[:, kc, :], start=(kc == 0), stop=(kc == 7)),
                         reads=[k, ("A:halo", 0)], writes=psk(b2))
            P.act(lambda e: e.activation(out=A0[:, 16:528], in_=C.PS[:, b, :], func=AF.Identity, bias=bcol), reads=psk(b) + [("A:binfm", 0)], writes=[("A:a0", 1)])
            if blk == 0:
                P.dve(lambda e: e.memset(A0[:, 0:16], 0.0), writes=[("A:a0", 0)])
            else:
                P.act(lambda e: e.activation(out=A0[:, 0:16], in_=C.PS[:, b2, 0:16], func=AF.Identity, bias=bcol), reads=psk(b2) + [("A:binfm", 0)], writes=[("A:a0", 0)])
            cur, ckey = A0, [("A:a0", 0), ("A:a0", 1)]
            for si, stp in enumerate([1, 2, 4, 8][:gi + 1]):
                nxt = A1 if si % 2 == 0 else A2
                nkey = ("A:a%d" % (1 if si % 2 == 0 else 2), 0)
                P.dve(lambda e, cur=cur, nxt=nxt, stp=stp: e.tensor_tensor(out=nxt[:, stp:528], in0=cur[:, stp:528], in1=cur[:, 0:528 - stp], op=ALU.add),
                      reads=ckey, writes=[nkey])
                cur, ckey = nxt, [nkey]
            if blk == 0:
                P.dve(lambda e, cur=cur: e.tensor_tensor(out=cur[:, 16:32], in0=cur[:, 16:32], in1=pfix[:, gi, :], op=ALU.mult), reads=ckey + [("A:pfix", 0)], writes=ckey)
            P.dve(lambda e, cur=cur: e.scalar_tensor_tensor(out=pooled, in0=cur[:, 16:528], scalar=1.0 / (2 << gi), in1=A0[:, 16:528], op0=ALU.mult, op1=ALU.subtract),
                  reads=ckey + [("A:a0", 1)], writes=[("A:pooled", 0)])

        def pool_back(gi):
            b = next_ps(C)
            P.pe(lambda e: e.matmul(C.PS[:, b, :], lhsT=pw[:, gi, :], rhs=pooled, start=True, stop=True), reads=[("A:pw", 0), ("A:pooled", 0)], writes=psk(b))
            P.act(lambda e: e.activation(out=poolmix[:, gi, :], in_=C.PS[:, b, :], func=AF.Identity, scale=pscale[:, gi:gi + 1]),
                  reads=psk(b) + [("A:pscale", 0)], writes=[("A:poolmix", gi)])

        def gate_proj(brn, j, hT=hT, hkeys=hkeys):
            ci = CI_MG + brn * 8 + j
            fm_proj(C, fm[ci], hT, hkeys, 512, binfm[:, ci:ci + 1], AF.Sigmoid, gsig[j % 2][brn], ("A:gsig", j % 2, brn))

        def gm_front_pair(i0_, hT=hT, hkeys=hkeys):
            bs_ = []
            for i in (i0_, i0_ + 1):
                d = i % 2
                b = next_ps(C)
                for kc in range(8):
                    P.pe(lambda e, kc=kc, b=b, i=i: e.matmul(C.PS[:, b, :], lhsT=hT[:, kc, i * 128:(i + 1) * 128], rhs=wgv[:, kc, :], start=(kc == 0), stop=False),
                         reads=[hkeys[i], ("A:wgv", 0)], writes=psk(b))
                P.pe(lambda e, b=b: e.matmul(C.PS[:, b, :], lhsT=C.e0[:, :], rhs=btm[:, TM_GV:TM_GV + 512], start=False, stop=True), reads=["e0", ("A:btm", 0)], writes=psk(b))
                bs_.append(b)
            for i, b in zip((i0_, i0_ + 1), bs_):
                d = i % 2
                P.act(lambda e, b=b, d=d: e.activation(out=gvf[d], in_=C.PS[:, b, :], func=AF.Gelu_apprx_tanh), reads=psk(b), writes=[("A:gvf", d)])
            for d in range(2):
                P.dve(lambda e, d=d: e.bn_stats(out=bst[:, d, 0:6], in_=gvf[d]), reads=[("A:gvf", d)], writes=[("A:bst", d)])
                P.dve(lambda e, d=d: e.bn_aggr(out=bst[:, d, 6:8], in_=bst[:, d, 0:6]), reads=[("A:bst", d)], writes=[("A:bst", d)])
            sk2 = [("A:bst", 0), ("A:bst", 1)]
            P.act(lambda e: e.activation(out=bst[:, :, 7:8], in_=bst[:, :, 7:8], func=AF.Sqrt, bias=C.epsb[:, 0:1]), reads=sk2 + ["epsb"], writes=sk2)
            P.dve(lambda e: e.reciprocal(out=bst[:, :, 7:8], in_=bst[:, :, 7:8]), reads=sk2, writes=sk2)
            for d in range(2):
                g_, v_, s_ = gvf[d], vn[d], bst[:, d, :]
                gk, vk, sk = ("A:gvf", d), ("A:vn", d), ("A:bst", d)
                P.dve(lambda e, g_=g_, s_=s_: e.tensor_scalar(out=g_, in0=g_, scalar1=s_[:, 6:7], scalar2=s_[:, 7:8], op0=ALU.subtract, op1=ALU.mult), reads=[gk, sk], writes=[gk])
                P.dve(lambda e, g_=g_: e.tensor_tensor(out=g_, in0=g_, in1=lng, op=ALU.mult), reads=[gk, ("A:lng", 0)], writes=[gk])
                P.dve(lambda e, g_=g_, v_=v_: e.tensor_tensor(out=v_, in0=g_, in1=lnb, op=ALU.add), reads=[gk, ("A:lnb", 0)], writes=[vk])

        def gm_back(i):
            d = i % 2
            b = next_ps(C)
            v_, t_ = vn[d], tA[d]
            for g in range(4):
                P.pe(lambda e, g=g: e.matmul(C.PS[:, b, g * 128:(g + 1) * 128], lhsT=v_[:, g * 128:(g + 1) * 128], rhs=gws[:, g, :], start=True, stop=True),
                     reads=[("A:vn", d), ("A:gws", 0)], writes=psk(b))
            P.dve(lambda e: e.tensor_tensor(out=t_, in0=C.PS[:, b, :], in1=bsb, op=ALU.add), reads=psk(b) + [("A:bsb", 0)], writes=[("A:tA", d)])
            uv = uT[:, :, i * 128:(i + 1) * 128]
            P.dve(lambda e: e.tensor_tensor(out=uv, in0=uv, in1=t_.rearrange("p (g t) -> p g t", t=128), op=ALU.mult),
                  reads=[("A:tA", d)] + [("A:uT", c) for c in range(4)], writes=[("A:gm", i)])

        def u_proj(c, hT=hT, hkeys=hkeys, blk=blk):
            fm_proj(C, fm[CI_U + c], hT, hkeys, 512, binfm[:, CI_U + c:CI_U + c + 1], AF.Gelu_apprx_tanh, uT[:, c, :], ("A:uT", c), pkey=("u", l, blk, c))

        pool_front(0)
        late_loads()
        u_proj(0)
        u_proj(1)
        pool_back(0)
        pool_front(1)
        gm_front_pair(0)
        pool_back(1)
        pool_front(2)
        u_proj(2)
        u_proj(3)
        gm_back(0)
        gm_back(1)
        pool_back(2)
        pool_front(3)
        for c in (0, 1, 2):
            gate_proj(c, 0)
        gm_front_pair(2)
        pool_back(3)
        for c in (0, 1, 2):
            gate_proj(c, 1)
        gm_back(2)
        gm_back(3)
        gmk = [("A:gm", i) for i in range(4)] + [("A:uT", c) for c in range(4)]
        branches = [(poolmix, [("A:poolmix", gi) for gi in range(4)]), (nsa_oT, nsak), (uT, gmk)]
        for j in range(8):
            tj = tA[0]
            tb = tA[1]
            for brn in range(3):
                src, skeys = branches[brn]
                a, k = load_chunk(C, dram[pre + "w_br"][brn * 8 + j], "wbr")
                b = next_ps(C)
                for kc in range(4):
                    P.pe(lambda e, kc=kc, b=b, a=a, src=src: e.matmul(C.PS[:, b, :], lhsT=a[:, kc, :], rhs=src[:, kc, :], start=(kc == 0), stop=(kc == 3)),
                         reads=[k] + skeys, writes=psk(b))
                gs = gsig[j % 2][brn]
                gk = ("A:gsig", j % 2, brn)
                if brn == 0:
                    P.dve(lambda e, b=b, gs=gs, tj=tj: e.tensor_tensor(out=tj, in0=C.PS[:, b, :], in1=gs, op=ALU.mult), reads=psk(b) + [gk], writes=[("A:tA", 0)])
                elif brn == 1:
                    P.dve(lambda e, b=b, gs=gs, tb=tb: e.tensor_tensor(out=tb, in0=C.PS[:, b, :], in1=gs, op=ALU.mult), reads=psk(b) + [gk], writes=[("A:tA", 1)])
                    P.dve(lambda e, tj=tj, tb=tb: e.tensor_tensor(out=tj, in0=tj, in1=tb, op=ALU.add), reads=[("A:tA", 0), ("A:tA", 1)], writes=[("A:tA", 0)])
                else:
                    P.dve(lambda e, b=b, gs=gs, tb=tb: e.tensor_tensor(out=tb, in0=C.PS[:, b, :], in1=gs, op=ALU.mult), reads=psk(b) + [gk], writes=[("A:tA", 1)])
                    P.dve(lambda e, tj=tj, tb=tb, j=j, merged=merged: e.tensor_tensor(out=merged[:, j, :], in0=tj, in1=tb, op=ALU.add), reads=[("A:tA", 0), ("A:tA", 1)], writes=[("A:merged", j)])
            if j + 2 < 8:
                for brn in range(3):
                    gate_proj(brn, j + 2)
        mk_ = [("A:merged", j) for j in range(8)]
        for i in range(4):
            b = next_ps(C, 2)
            for hf in range(2):
                for j in range(8):
                    P.pe(lambda e, j=j, hf=hf, b=b, i=i, merged=merged: e.matmul(C.PS[:, b + hf, :], lhsT=merged[:, j, i * 128:(i + 1) * 128], rhs=wmo[:, j, hf * 512:(hf + 1) * 512], start=(j == 0), stop=(j == 7)),
                         reads=mk_ + [("A:wmo", 0), ("A:wmo", 1)], writes=psk(b + hf))
            post_norm_update(C, b, blk * 4 + i, gpost, gpkey)


def cross_phase(C, l):
    P, dram = C.P, C.dram
    pre = "L%d_" % l
    C.ps_lo, C.ps_hi = 0, 4
    for c in range(2):
        load_chunk(C, dram[pre + "wk"][c], "wk", pkey=(pre, "wk", c), prefetch=True)
    P.barrier(lambda e: e.memset(C.bar[:], 0.0))
    wq = carve(C, 0, [128, 8, 8, 128], BF16)
    wo = carve(C, 16384, [128, 8, 1024], BF16)
    wv = carve(C, 32768, [128, 8, 1024], BF16)
    kT = carve(C, 49152, [128, 8, 256], BF16)
    Va = carve(C, 53248, [128, 2, 4, 257], BF16)
    memT = carve(C, 57376, [128, 8, 256], BF16)
    hTb = [carve(C, 61472 + i * 8192, [128, 8, 512], BF16) for i in range(2)]
    qT = carve(C, 77856, [128, 8, 512], BF16)
    E = carve(C, 86048, [128, 2, 4, 512], BF16)
    otok = carve(C, 94240, [128, 4, 256], BF16)
    oT = carve(C, 96288, [128, 8, 128], BF16)
    memf = carve(C, 98336, [128, 2, 1024], F32)
    wqd = dram[pre + "wq"].rearrange("c p k j -> p c k j")
    P.dma("sp", lambda e: e.dma_start(out=memf, in_=dram["mem"].rearrange("(t p) d -> p t d", p=128)), writes=[("A:memf", 0)])
    kcol, kkey, _, _ = load_gains(C, dram[pre + "mem_kvg"], dram[pre + "mem_post"], 1.0)
    gcol, gkey, gpost, gpkey = load_gains(C, dram[pre + "mem_pre"], dram[pre + "mem_post"], 1.0)
    P.dve(lambda e: e.memset(Va[:, :, :, 256:257], 1.0), writes=[("A:Vones", 0)])
    rms_stats(C, [memf[:, mt, :] for mt in range(2)], [("A:memf", 0)] * 2, 16)
    for mt in range(2):
        norm_transpose(C, memf[:, mt, :], ("A:memf", 0), 16 + mt, kcol, kkey, memT[:, :, mt * 128:(mt + 1) * 128], ("A:memT", mt))
    rms_stats(C, [C.X[:, t, :] for t in range(NT)], [("X", t) for t in range(NT)], 0)
    mk = [("A:memT", 0), ("A:memT", 1)]
    for c in range(8):
        a, k = load_chunk(C, dram[pre + "wk"][c], "wk", pkey=(pre, "wk", c))
        if c == 4:
            for hf in range(2):
                P.dma("pool", lambda e, hf=hf: e.dma_start(out=wv[:, hf * 4:(hf + 1) * 4, :], in_=dram[pre + "wv"][:, hf * 4:(hf + 1) * 4, :]), writes=[("A:wv", hf)])
        b = next_ps(C)
        for kc in range(8):
            P.pe(lambda e, kc=kc, a=a, b=b: e.matmul(C.PS[:, b, 0:256], lhsT=a[:, kc, :], rhs=memT[:, kc, :], start=(kc == 0), stop=(kc == 7)),
                 reads=[k] + mk, writes=psk(b))
        P.act(lambda e, b=b, c=c: e.copy(out=kT[:, c, :], in_=C.PS[:, b, 0:256]), reads=psk(b), writes=[("A:kT", c)])
    for hf in range(2):
        P.dma("pool", lambda e, hf=hf: e.dma_start(out=wq[:, hf * 4:(hf + 1) * 4], in_=wqd[:, hf * 4:(hf + 1) * 4]), writes=[("A:wq", hf)])
    for hf in range(2):
        P.dma("pool", lambda e, hf=hf: e.dma_start(out=wo[:, hf * 4:(hf + 1) * 4, :], in_=dram[pre + "wo"][:, hf * 4:(hf + 1) * 4, :]), writes=[("A:wo", hf)])
    for mt in range(2):
        for hf in range(2):
            b = next_ps(C)
            for kc in range(8):
                P.pe(lambda e, kc=kc, b=b, mt=mt, hf=hf: e.matmul(C.PS[:, b, :], lhsT=memT[:, kc, mt * 128:(mt + 1) * 128], rhs=wv[:, kc, hf * 512:(hf + 1) * 512], start=(kc == 0), stop=(kc == 7)),
                     reads=[("A:memT", mt), ("A:wv", 0), ("A:wv", 1)], writes=psk(b))
            P.dve(lambda e, b=b, mt=mt, hf=hf: e.tensor_copy(out=Va[:, mt, 2 * hf:2 * hf + 2, 0:256], in_=C.PS[:, b, :].rearrange("p (h d) -> p h d", d=256)),
                  reads=psk(b), writes=[("A:Va", mt, hf)])
    vkeys = [("A:Va", mt, hf) for mt in range(2) for hf in range(2)] + [("A:Vones", 0)]

    def do_norm(blk):
        hT = hTb[blk % 2]
        hkeys = [("A:hT", blk % 2, i) for i in range(4)]
        for i in range(4):
            t = blk * 4 + i
            norm_transpose(C, C.X[:, t, :], ("X", t), t, gcol, gkey, hT[:, :, i * 128:(i + 1) * 128], hkeys[i])
        return hT, hkeys

    def out_proj(t):
        b = next_ps(C, 2)
        for hf in range(2):
            for kc in range(8):
                P.pe(lambda e, kc=kc, hf=hf, b=b: e.matmul(C.PS[:, b + hf, :], lhsT=oT[:, kc, :], rhs=wo[:, kc, hf * 512:(hf + 1) * 512], start=(kc == 0), stop=(kc == 7)),
                     reads=[("A:oT", 0), ("A:wo", 0), ("A:wo", 1)], writes=psk(b + hf))
        post_norm_update(C, b, t, gpost, gpkey)

    nxt = do_norm(0)
    pending = None
    for blk in range(4):
        tiles = list(range(blk * 4, blk * 4 + 4))
        hT, hkeys = nxt
        for c in range(8):
            b = next_ps(C)
            for kc in range(8):
                P.pe(lambda e, kc=kc, b=b, c=c, hT=hT: e.matmul(C.PS[:, b, :], lhsT=wq[:, c, kc, :], rhs=hT[:, kc, :], start=(kc == 0), stop=(kc == 7)),
                     reads=[("A:wq", c // 4)] + hkeys, writes=psk(b))
            P.dve(lambda e, b=b, c=c: e.tensor_copy(out=qT[:, c, :], in_=C.PS[:, b, :]), reads=psk(b), writes=[("A:qT", c)])
        if pending is not None:
            out_proj(pending)
            pending = None
        for h in range(4):
            for mt in range(2):
                b = next_ps(C)
                for j in range(2):
                    P.pe(lambda e, b=b, h=h, mt=mt, j=j: e.matmul(C.PS[:, b, :], lhsT=kT[:, 2 * h + j, mt * 128:(mt + 1) * 128], rhs=qT[:, 2 * h + j, :], start=(j == 0), stop=(j == 1)),
                         reads=[("A:kT", 2 * h + j), ("A:qT", 2 * h + j)], writes=psk(b))
                P.act(lambda e, b=b, h=h, mt=mt: e.activation(out=E[:, mt, h, :], in_=C.PS[:, b, :], func=AF.Exp, scale=1.0 / 16.0),
                      reads=psk(b), writes=[("A:E", mt, h)])
        for i, t in enumerate(tiles):
            for h in range(4):
                for mt in range(2):
                    P.pe(lambda e, h=h, mt=mt, i=i: e.matmul(C.PS[:, 4 + h, 0:257], lhsT=E[:, mt, h, i * 128:(i + 1) * 128], rhs=Va[:, mt, h, :], start=(mt == 0), stop=(mt == 1)),
                         reads=[("A:E", mt, h)] + vkeys, writes=psk(4 + h))
            P.dve(lambda e: e.reciprocal(out=C.small[:, 0:4], in_=C.PS[:, 4:8, 256:257].rearrange("p h o -> p (h o)")), reads=psk(4, 4), writes=["rinv"])
            P.dve(lambda e: e.tensor_tensor(out=otok, in0=C.PS[:, 4:8, 0:256], in1=C.small[:, 0:4].unsqueeze(2).broadcast_to([128, 4, 256]), op=ALU.mult),
                  reads=psk(4, 4) + ["rinv"], writes=[("A:otok", 0)])
            if pending is not None:
                out_proj(pending)
            if i == 1 and blk < 3:
                nxt = do_norm(blk + 1)
            b = next_ps(C)
            pv = C.PS[:, b, :].bitcast(BF16)
            of = otok.rearrange("p h d -> p (h d)")
            for kc in range(8):
                P.pe(lambda e, kc=kc, pv=pv: e.transpose(out=pv[:, kc * 128:(kc + 1) * 128], in_=of[:, kc * 128:(kc + 1) * 128], identity=C.ident[:]),
                     reads=[("A:otok", 0), "ident"], writes=psk(b))
            P.act(lambda e, pv=pv: e.copy(out=oT, in_=pv.rearrange("p (k t) -> p k t", t=128)), reads=psk(b), writes=[("A:oT", 0)])
            pending = t
    out_proj(pending)


_CACHE = {}


def kernel(**inputs):
    inp = {k: np.asarray(v) for k, v in inputs.items()}
    n_phase = _CACHE.get("n_phase")
    consts = host_consts()
    lay = [layer_inputs(inp, l) for l in range(2)]
    shapes = {k: v.shape for k, v in lay[0].items()}
    nc = build_program(shapes, n_phase=n_phase)
    in_maps = []
    for b in range(8):
        m = {"x": np.ascontiguousarray(inp["x"][b]), "mem": np.ascontiguousarray(inp["mem"][b])}
        for k, v in consts.items():
            m["c_" + k] = v
        for l in range(2):
            for k, v in lay[l].items():
                m["L%d_%s" % (l, k)] = v
        in_maps.append(m)
    res = run_bass_kernel_spmd(nc, in_maps, core_ids=list(range(8)))
    return np.stack([r["out"] for r in res.results], axis=0).astype(np.float32)
```

```python
import contextlib
import ml_dtypes
import numpy as np
import concourse.bass as bass
import concourse.mybir as mybir
from concourse.bass_utils import run_bass_kernel_spmd

F32 = mybir.dt.float32
BF16 = mybir.dt.bfloat16
AF = mybir.ActivationFunctionType
ALU = mybir.AluOpType

ENGS = ("pe", "act", "dve", "pool", "sp")
N_RING = 24
EPS = 1e-6
S = 2048
D = 1024
DFF = 2816
NT = 16
NEGM = -30000.0

CI_POOL, CI_Q, CI_KCMP, CI_VCMP, CI_KSLC, CI_KWIN, CI_U, CI_MG = 0, 4, 8, 9, 10, 12, 14, 18
N_FM = 42
TM_V, TM_G, TM_GV, N_TM = 0, 256, 280, 792


class Op:
    __slots__ = ("eng", "fn", "deps", "dma", "idx", "event", "signal", "ring_dep")

    def __init__(self, eng, fn, deps, dma, idx):
        self.eng, self.fn, self.deps, self.dma, self.idx = eng, fn, deps, dma, idx
        self.event = None
        self.signal = False
        self.ring_dep = None


class Prog:
    def __init__(self, nc):
        self.nc = nc
        self.ops = []
        self.last_w = {}
        self.readers = {}
        self.final_dma = []
        self.barrier_idx = None
        self.arena_keys = set()

    @staticmethod
    def _is_arena(k):
        return isinstance(k, tuple) and isinstance(k[0], str) and k[0].startswith("A:")

    def op(self, eng, fn, reads=(), writes=(), dma=False, final=False):
        idx = len(self.ops)
        deps = set()
        for r in list(reads) + list(writes):
            if r not in self.last_w and self._is_arena(r) and self.barrier_idx is not None:
                deps.add(self.barrier_idx)
            if self._is_arena(r):
                self.arena_keys.add(r)
        for r in reads:
            w = self.last_w.get(r)
            if w is not None:
                deps.add(w)
        for r in writes:
            w = self.last_w.get(r)
            if w is not None:
                deps.add(w)
            for rd in self.readers.get(r, ()):
                deps.add(rd)
        deps.discard(idx)
        o = Op(eng, fn, deps, dma, idx)
        self.ops.append(o)
        for r in reads:
            self.readers.setdefault(r, []).append(idx)
        for r in writes:
            self.last_w[r] = idx
            self.readers[r] = []
        if final:
            self.final_dma.append(idx)
        return idx

    def barrier(self, fn):
        keys = list(self.arena_keys)
        idx = self.op("dve", fn, reads=(), writes=keys)
        for k in keys:
            self.last_w.pop(k, None)
            self.readers.pop(k, None)
        self.arena_keys = set()
        self.barrier_idx = idx
        return idx

    def pe(self, fn, reads=(), writes=()):
        return self.op("pe", fn, reads, writes)

    def act(self, fn, reads=(), writes=()):
        return self.op("act", fn, reads, writes)

    def dve(self, fn, reads=(), writes=()):
        return self.op("dve", fn, reads, writes)

    def pool(self, fn, reads=(), writes=()):
        return self.op("pool", fn, reads, writes)

    def dma(self, eng, fn, reads=(), writes=(), final=False):
        return self.op(eng, fn, reads, writes, dma=True, final=final)

    def emit(self, stack):
        nc = self.nc
        ops = self.ops
        for o in ops:
            for d in o.deps:
                p = ops[d]
                if p.dma:
                    p.signal = True
                elif p.eng == "pe" and o.eng == "pe":
                    continue
                else:
                    p.signal = True
        for i in self.final_dma:
            ops[i].signal = True
        sem_eng = {e: stack.enter_context(nc.semaphore("s_" + e)) for e in ENGS}
        rings = {e: [stack.enter_context(nc.semaphore("r_%s%d" % (e, i))) for i in range(N_RING)]
                 for e in ("sp", "act", "pool")}
        cnt = {e: 0 for e in ENGS}
        dcnt = {e: 0 for e in rings}
        dma_hist = {e: [] for e in rings}
        for o in ops:
            if o.dma:
                k = dcnt[o.eng]
                dcnt[o.eng] += 1
                o.event = (rings[o.eng][k % N_RING], 16 * (k // N_RING + 1), 16)
                if k >= N_RING:
                    o.ring_dep = dma_hist[o.eng][k - N_RING]
                dma_hist[o.eng].append(o.event)
            elif o.signal:
                cnt[o.eng] += 1
                o.event = (sem_eng[o.eng], cnt[o.eng], 1)
        block = stack.enter_context(nc.Block())

        def run_engine(ename, eng):
            waited = {}
            for o in ops:
                if o.eng != ename:
                    continue
                evs = []
                for d in o.deps:
                    p = ops[d]
                    if p.event is None:
                        continue
                    if p.eng == "pe" and ename == "pe" and not p.dma:
                        continue
                    evs.append(p.event)
                if o.ring_dep is not None:
                    evs.append(o.ring_dep)
                need = {}
                for (s, v, _) in evs:
                    key = id(s)
                    if waited.get(key, 0) >= v:
                        continue
                    if key not in need or need[key][1] < v:
                        need[key] = (s, v)
                for key, (s, v) in need.items():
                    eng.wait_ge(s, v)
                    waited[key] = v
                ins = o.fn(eng)
                if o.event is not None:
                    ins.then_inc(o.event[0], o.event[2])
            if ename == "sp":
                for i in self.final_dma:
                    s, v, _ = ops[i].event
                    eng.wait_ge(s, v)

        @block.tensor
        def _(e):
            run_engine("pe", e)

        @block.scalar
        def _(e):
            run_engine("act", e)

        @block.vector
        def _(e):
            run_engine("dve", e)

        @block.gpsimd
        def _(e):
            run_engine("pool", e)

        @block.sync
        def _(e):
            run_engine("sp", e)


def lay_lhsT(W):
    K, N = W.shape
    return np.ascontiguousarray(W.reshape(K // 128, 128, N // 128, 128).transpose(2, 1, 0, 3))


def lay_rhs(W):
    K, N = W.shape
    return np.ascontiguousarray(W.reshape(K // 128, 128, N).transpose(1, 0, 2))


def lay_col(v):
    return np.ascontiguousarray(v.reshape(-1, 128).T)


def host_consts():
    c = {}
    c["ident"] = np.eye(128, dtype=np.float32)
    n = np.arange(127)[:, None]
    t = np.arange(S)[None, :]
    c["cmpmask"] = (16 * n + 31 <= t).astype(np.float32)
    cs = np.arange(127)[:, None] * 16
    ss = np.arange(32)[None, :] * 64
    ov = np.minimum(cs + 32, ss + 64) - np.maximum(cs, ss)
    c["ov"] = (np.maximum(ov, 0) / 32).astype(np.float32)
    c["eall"] = (np.arange(S)[None, :] // 64 == np.arange(32)[:, None]).astype(np.float32)
    p = np.arange(128)[:, None]
    q = np.arange(512)[None, :]
    caus = np.stack([np.where(q - (r * 128 + p) >= 0, 0.0, NEGM) for r in range(4)], 1)
    c["causneg"] = caus.astype(np.float32)
    win = np.stack([np.where((q - (r * 128 + p) >= 0) & (q - (r * 128 + p) < 256), 0.0, NEGM)
                    for r in range(-2, 4)], 1)
    c["winneg"] = win.astype(np.float32)
    pos = (np.arange(NT)[None, :] * 128 + np.arange(128)[:, None])
    cur = pos // 64
    j = np.arange(32)[None, None, :]
    forced = (j == 0) | (j == cur[:, :, None]) | (j == cur[:, :, None] - 1)
    valid = j * 64 <= pos[:, :, None]
    c["vmask"] = (valid & ~forced).astype(np.float32)
    c["addterm"] = np.where(forced, 1e9, np.where(valid, 0.0, -1.0)).astype(np.float32)
    fix = np.ones((128, 4, 16), np.float32)
    for gi, w in enumerate((2, 4, 8, 16)):
        tt = np.arange(16)
        fix[:, gi, :] = (w / np.minimum(tt + 1, w))[None, :]
    c["poolfix"] = fix
    c["trilT"] = (np.arange(128)[:, None] <= np.arange(128)[None, :]).astype(np.float32)
    for k in BF16_CONSTS:
        c[k] = c[k].astype(ml_dtypes.bfloat16)
    return c


BF16_CONSTS = ("cmpmask", "eall", "causneg", "winneg")
CONST_SHAPES = {"ident": [128, 128], "cmpmask": [127, S], "ov": [127, 32], "eall": [32, S],
                "causneg": [128, 4, 512], "winneg": [128, 6, 512], "vmask": [128, NT, 32],
                "addterm": [128, NT, 32], "poolfix": [128, 4, 16], "trilT": [128, 128]}


def layer_inputs(inp, l):
    d = {}
    for f in ("ff1", "ff2"):
        d[f + "_w1"] = lay_lhsT(inp[f + "_w1"][l])
        d[f + "_w3"] = lay_lhsT(inp[f + "_w3"][l])
        d[f + "_w2"] = lay_rhs(inp[f + "_w2"][l])
        d[f + "_pre"] = lay_col(inp[f + "_pre_g"][l])
        d[f + "_post"] = np.ascontiguousarray(inp[f + "_post_g"][l])
    W = inp["w_in"][l]
    b = inp["b_in"][l]
    dup = lambda a: np.concatenate([a, a], axis=-1)
    cols = [W[:, 0:512], W[:, 512:1024], W[:, 1024:1152], W[:, 1152:1280],
            dup(W[:, 1280:1344]), dup(W[:, 1344:1408]), dup(W[:, 1536:1600]), dup(W[:, 1600:1664]),
            W[:, 1816:2328], W[:, 2840:5912]]
    bcols = [b[0:512], b[512:1024], b[1024:1152], b[1152:1280],
             dup(b[1280:1344]), dup(b[1344:1408]), dup(b[1536:1600]), dup(b[1600:1664]),
             b[1816:2328], b[2840:5912]]
    d["win_fm"] = lay_lhsT(np.concatenate(cols, axis=1))
    d["bin_fm"] = lay_col(np.concatenate(bcols))
    tm = np.concatenate([W[:, 1408:1536], W[:, 1664:1792], W[:, 1792:1816], W[:, 2328:2840]], axis=1)
    d["win_tm"] = lay_rhs(tm)
    d["bin_tm"] = np.ascontiguousarray(np.concatenate([b[1408:1536], b[1664:1792], b[1792:1816], b[2328:2840]])[None, :])
    d["mix_pre"] = lay_col(inp["mix_pre_g"][l])
    d["mix_post"] = np.ascontiguousarray(inp["mix_post_g"][l])
    d["pool_w"] = np.ascontiguousarray(inp["pool_w"][l].transpose(1, 0, 2))
    d["pool_scale"] = lay_col(inp["pool_scale"][l])
    for kv in ("k", "v"):
        w1 = inp["cmp_w1_" + kv][l]
        w1p = w1.transpose(1, 0, 2)
        d["cw1_" + kv] = np.ascontiguousarray(np.concatenate([w1p, w1p], axis=0))
        pT = inp["cmp_pos_" + kv][l].T
        d["cpos_" + kv] = np.ascontiguousarray(np.concatenate([pT, pT], axis=0))
    w2k = inp["cmp_w2_k"][l]
    d["cw2_k"] = lay_rhs(dup(w2k))
    d["cw2_v"] = lay_rhs(inp["cmp_w2_v"][l])
    d["ln_g"] = np.ascontiguousarray(inp["gmlp_ln_g"][l])
    d["ln_b"] = np.ascontiguousarray(inp["gmlp_ln_b"][l])
    d["gws"] = np.ascontiguousarray(inp["gmlp_ws"][l].transpose(2, 0, 1))
    d["gbs"] = np.ascontiguousarray(inp["gmlp_bs"][l].reshape(-1))
    wbr = np.concatenate([inp["w_br_pool"][l], inp["w_br_nsa"][l], inp["w_br_gmlp"][l]], axis=1)
    d["w_br"] = lay_lhsT(wbr)
    d["w_mo"] = lay_rhs(inp["w_mix_out"][l])
    d["mem_pre"] = lay_col(inp["mem_pre_g"][l])
    d["mem_kvg"] = lay_col(inp["mem_kv_g"][l])
    d["mem_post"] = np.ascontiguousarray(inp["mem_post_g"][l])
    d["wq"] = lay_lhsT(inp["mem_wq"][l])
    d["wk"] = lay_lhsT(inp["mem_wk"][l])
    d["wv"] = lay_rhs(inp["mem_wv"][l])
    d["wo"] = lay_rhs(inp["mem_wo"][l])
    return d


class Ctx:
    pass


def build_program(layer_shapes, n_phase=None, depth=2):
    nc = bass.Bass("TRN2", target_bir_lowering=False)
    C = Ctx()
    C.nc = nc
    dram = {}
    dram["x"] = nc.dram_tensor("x", [S, D], F32, kind="ExternalInput").ap()
    dram["mem"] = nc.dram_tensor("mem", [256, D], F32, kind="ExternalInput").ap()
    for k, shp in CONST_SHAPES.items():
        dram["c_" + k] = nc.dram_tensor("c_" + k, shp, BF16 if k in BF16_CONSTS else F32, kind="ExternalInput").ap()
    for l in range(depth):
        for k, shp in layer_shapes.items():
            nm = "L%d_%s" % (l, k)
            dram[nm] = nc.dram_tensor(nm, list(shp), F32, kind="ExternalInput").ap()
    out = nc.dram_tensor("out", [S, D], F32, kind="ExternalOutput").ap()
    C.dram = dram

    with contextlib.ExitStack() as st:
        sb = lambda n, s, dt: st.enter_context(nc.sbuf_tensor(n, s, dt))
        P = Prog(nc)
        C.P = P
        C.X = sb("X", [128, NT, D], F32)
        ARENA_BYTES = 114 * 1024
        C.arena_bytes = ARENA_BYTES
        C.arena = sb("arena", [128, ARENA_BYTES // 2], BF16)
        C.PS = st.enter_context(nc.psum_tensor("PS", [128, 8, 512], F32))
        C.wring = sb("wring", [128, 5, 8, 128], BF16)
        C.wring_i = 0
        C.ident = sb("ident", [128, 128], BF16)
        C.zeros = sb("zeros", [128, 512], BF16)
        C.ones = sb("ones", [128, 128], BF16)
        C.junk = sb("junk", [128, D], BF16)
        C.hb = sb("hb", [128, 2, D], BF16)
        C.hb_i = 0
        C.ss = sb("ss", [128, 32], F32)
        C.rstd = sb("rstd", [128, 32], F32)
        C.gcol = sb("gcol", [128, 2, 8], F32)
        C.gcol_i = 0
        C.gpost = sb("gpost", [128, 1, D], F32)
        C.gpost_i = 0
        C.tmpf = sb("tmpf", [128, 1, D], F32)
        C.tmpf_i = 0
        C.ss2 = sb("ss2", [128, 4], F32)
        C.ss2_i = 0
        C.sg = sb("sg", [128, 2, 512], BF16)
        C.sg_i = 0
        C.ps_i = 0
        C.ps_lo, C.ps_hi = 0, 8
        C.small = sb("small", [128, 64], F32)
        C.epsb = sb("epsb", [128, 1], F32)
        C.bar = sb("bar", [128, 8], F32)
        P.dve(lambda e: e.memset(C.epsb[:], EPS), writes=["epsb"])

        P.dma("pool", lambda e: e.dma_start(out=C.ident[:], in_=dram["c_ident"]), writes=["ident"])
        P.dve(lambda e: e.memset(C.zeros[:], 0.0), writes=["zeros"])
        P.dve(lambda e: e.memset(C.ones[:], 1.0), writes=["ones"])
        C.e0 = sb("e0", [128, 128], BF16)
        P.dve(lambda e: e.memset(C.e0[:], 0.0), writes=["e0"])
        P.dve(lambda e: e.memset(C.e0[0:1, :], 1.0), writes=["e0"])
        xv = dram["x"].rearrange("(t p) d -> p t d", p=128)
        for t in range(NT):
            P.dma("sp", lambda e, t=t: e.dma_start(out=C.X[:, t, :], in_=xv[:, t, :]), writes=[("X", t)])

        phases = []
        for l in range(depth):
            phases += [("ffn", l, "ff1"), ("mixer", l, None), ("cross", l, None), ("ffn", l, "ff2")]
        if isinstance(n_phase, int):
            phases = phases[:n_phase]
        elif n_phase is not None:
            phases = list(n_phase)
        for (kind, l, which) in phases:
            if kind == "ffn":
                ffn_phase(C, l, which)
            elif kind == "mixer":
                mixer_phase(C, l)
            elif kind == "cross":
                cross_phase(C, l)

        ov = out.rearrange("(t p) d -> p t d", p=128)
        for t in range(NT):
            P.dma("sp", lambda e, t=t: e.dma_start(out=ov[:, t, :], in_=C.X[:, t, :]), reads=[("X", t)], final=True)
        P.emit(st)
    return nc


def carve(C, off, shape, dt):
    n = int(np.prod(shape[1:]))
    if dt == BF16:
        ap = C.arena[:, off // 2: off // 2 + n]
    else:
        ap = C.arena[:, off // 2: off // 2 + 2 * n].bitcast(F32)
    if len(shape) == 2:
        return ap
    names = "abcd"[: len(shape) - 1]
    pat = "p (%s) -> p %s" % (" ".join(names), " ".join(names))
    return ap.rearrange(pat, **{names[i]: shape[i + 1] for i in range(1, len(shape) - 1)})


def next_ps(C, n=1):
    i = C.ps_i
    if i < C.ps_lo or i >= C.ps_hi:
        i = C.ps_lo
    if n == 2:
        i = (i + 1) // 2 * 2
    if i + n > C.ps_hi:
        i = C.ps_lo
    C.ps_i = i + n
    return i


def psk(b, n=1):
    return [("ps", b + i) for i in range(n)]


def load_chunk(C, src_ap, tag, pkey=None, prefetch=False):
    if not hasattr(C, "prefetched"):
        C.prefetched = {}
    if pkey is not None and not prefetch and pkey in C.prefetched:
        return C.prefetched.pop(pkey)
    i = C.wring_i
    C.wring_i = (i + 1) % 5
    kc = src_ap.shape[1]
    dst = C.wring[:, i, 0:kc, :]
    C.P.dma("pool", lambda e: e.dma_start(out=dst, in_=src_ap), writes=[("wring", i)])
    if prefetch:
        C.prefetched[pkey] = (dst, ("wring", i))
    return dst, ("wring", i)


def load_gains(C, pre_ap, post_ap, post_scale):
    P = C.P
    gi = C.gcol_i
    C.gcol_i ^= 1
    pi = 0
    gcol = C.gcol[:, gi, :]
    gpost = C.gpost[:, pi, :]
    if pre_ap is not None:
        P.dma("sp", lambda e: e.dma_start(out=gcol, in_=pre_ap), writes=[("gcol", gi)])
    P.dma("sp", lambda e: e.dma_start(out=gpost, in_=post_ap.partition_broadcast(128)), writes=[("gpost", pi)])
    if post_scale != 1.0:
        P.dve(lambda e: e.tensor_scalar(out=gpost, in0=gpost, scalar1=post_scale, scalar2=None, op0=ALU.mult),
              reads=[("gpost", pi)], writes=[("gpost", pi)])
    return gcol, ("gcol", gi), gpost, ("gpost", pi)


def rms_stats(C, srcs, src_keys, col0):
    P = C.P
    n = len(srcs)
    for i, (s, k) in enumerate(zip(srcs, src_keys)):
        P.act(lambda e, s=s, i=i: e.activation(out=C.junk[:], in_=s, func=AF.Square, accum_out=C.ss[:, col0 + i: col0 + i + 1]),
              reads=[k], writes=["junk", ("ss", col0 + i)])
    cols = [("ss", col0 + i) for i in range(n)]
    rc = [("rstd", col0 + i) for i in range(n)]
    P.act(lambda e: e.activation(out=C.rstd[:, col0: col0 + n], in_=C.ss[:, col0: col0 + n], func=AF.Sqrt, scale=1.0 / D, bias=C.epsb[:, 0:1]),
          reads=cols + ["epsb"], writes=rc)
    P.dve(lambda e: e.reciprocal(out=C.rstd[:, col0: col0 + n], in_=C.rstd[:, col0: col0 + n]), reads=rc, writes=rc)


def norm_act(C, src, src_key, rcol):
    P = C.P
    hi = C.hb_i
    C.hb_i ^= 1
    hb = C.hb[:, hi, :]
    P.act(lambda e: e.activation(out=hb, in_=src, func=AF.Identity, scale=C.rstd[:, rcol: rcol + 1]),
          reads=[src_key, ("rstd", rcol)], writes=[("hb", hi)])
    return hi


def norm_pe(C, hi, gcol, gkey, dst, dst_key):
    P = C.P
    hb = C.hb[:, hi, :]
    b = next_ps(C)
    pv = C.PS[:, b, :].bitcast(BF16)
    for kc in range(8):
        P.pe(lambda e, kc=kc: e.transpose(out=pv[:, kc * 128:(kc + 1) * 128], in_=hb[:, kc * 128:(kc + 1) * 128], identity=C.ident[:]),
             reads=[("hb", hi), "ident"], writes=psk(b))
    P.dve(lambda e: e.tensor_tensor(out=dst, in0=pv.rearrange("p (k t) -> p k t", t=128),
                                    in1=gcol.unsqueeze(2).broadcast_to([128, 8, 128]), op=ALU.mult),
          reads=psk(b) + [gkey], writes=[dst_key])


def norm_transpose(C, src, src_key, rcol, gcol, gkey, dst, dst_key):
    hi = norm_act(C, src, src_key, rcol)
    norm_pe(C, hi, gcol, gkey, dst, dst_key)


def post_norm_update(C, b, t, gpost, gpkey, extra_reads=()):
    P = C.P
    y = C.PS[:, b:b + 2, :]
    si = C.ss2_i
    C.ss2_i = (si + 1) % 4
    ti = 0
    sc = C.ss2[:, si:si + 1]
    tmp = C.tmpf[:, ti, :]
    P.act(lambda e: e.activation(out=C.junk[:].rearrange("p (a b) -> p a b", b=512), in_=y, func=AF.Square, accum_out=sc),
          reads=psk(b, 2), writes=["junk", ("ss2", si)])
    P.act(lambda e: e.activation(out=sc, in_=sc, func=AF.Sqrt, scale=1.0 / D, bias=C.epsb[:, 0:1]),
          reads=[("ss2", si)], writes=[("ss2", si)])
    P.dve(lambda e: e.reciprocal(out=sc, in_=sc), reads=[("ss2", si)], writes=[("ss2", si)])
    P.dve(lambda e: e.scalar_tensor_tensor(out=tmp.rearrange("p (a b) -> p a b", b=512), in0=y, scalar=sc,
                                           in1=gpost.rearrange("p (a b) -> p a b", b=512), op0=ALU.mult, op1=ALU.mult),
          reads=psk(b, 2) + [("ss2", si), gpkey], writes=[("tmpf", ti)])
    P.dve(lambda e: e.tensor_tensor(out=C.X[:, t, :], in0=C.X[:, t, :], in1=tmp, op=ALU.add),
           reads=[("tmpf", ti), ("X", t)], writes=[("X", t)])


def ffn_phase(C, l, which):
    P, dram = C.P, C.dram
    C.ps_lo, C.ps_hi = 0, 8
    pre = "L%d_%s_" % (l, which)
    load_chunk(C, dram[pre + "w1"][0], "w1", pkey=(pre, "w1", 0), prefetch=True)
    load_chunk(C, dram[pre + "w3"][0], "w3", pkey=(pre, "w3", 0), prefetch=True)
    P.barrier(lambda e: e.memset(C.bar[:], 0.0))
    h1T = carve(C, 0, [128, 22, 1024], BF16)
    w2s = carve(C, 45056, [128, 22, 1024], BF16)
    hT = carve(C, 90112, [128, 8, 1024], BF16)
    gcol, gkey, gpost, gpkey = load_gains(C, dram[pre + "pre"], dram[pre + "post"], 0.5)
    w1d, w3d = dram[pre + "w1"], dram[pre + "w3"]
    rms_stats(C, [C.X[:, t, :] for t in range(8)], [("X", t) for t in range(8)], 0)
    for i in range(8):
        norm_transpose(C, C.X[:, i, :], ("X", i), i, gcol, gkey, hT[:, :, i * 128:(i + 1) * 128], ("A:hT", i))
    rms_stats(C, [C.X[:, t, :] for t in range(8, NT)], [("X", t) for t in range(8, NT)], 8)
    for grp in range(2):
        tiles = list(range(grp * 8, grp * 8 + 8))
        for c in range(22):
            a1, k1 = load_chunk(C, w1d[c], "w1", pkey=(pre, "w1", c) if grp == 0 else None)
            a3, k3 = load_chunk(C, w3d[c], "w3", pkey=(pre, "w3", c) if grp == 0 else None)
            if grp == 0 and c == 1:
                for hf in range(2):
                    P.dma("pool", lambda e, hf=hf: e.dma_start(out=w2s[:, hf * 11:(hf + 1) * 11, :], in_=dram[pre + "w2"][:, hf * 11:(hf + 1) * 11, :]),
                          writes=[("A:w2", hf)])
            for blk in range(2):
                ba = next_ps(C)
                bb = next_ps(C)
                hk = [("A:hT", blk * 4 + i) for i in range(4)]
                for kc in range(8):
                    P.pe(lambda e, kc=kc, ba=ba, a1=a1, blk=blk: e.matmul(C.PS[:, ba, :], lhsT=a1[:, kc, :], rhs=hT[:, kc, blk * 512:(blk + 1) * 512], start=(kc == 0), stop=(kc == 7)),
                         reads=[k1] + hk, writes=psk(ba))
                for kc in range(8):
                    P.pe(lambda e, kc=kc, bb=bb, a3=a3, blk=blk: e.matmul(C.PS[:, bb, :], lhsT=a3[:, kc, :], rhs=hT[:, kc, blk * 512:(blk + 1) * 512], start=(kc == 0), stop=(kc == 7)),
                         reads=[k3] + hk, writes=psk(bb))
                si = C.sg_i
                C.sg_i ^= 1
                sg = C.sg[:, si, :]
                P.act(lambda e, ba=ba, sg=sg: e.activation(out=sg, in_=C.PS[:, ba, :], func=AF.Silu), reads=psk(ba), writes=[("sg", si)])
                P.dve(lambda e, bb=bb, sg=sg, c=c, blk=blk: e.tensor_tensor(out=h1T[:, c, blk * 512:(blk + 1) * 512], in0=C.PS[:, bb, :], in1=sg, op=ALU.mult),
                      reads=psk(bb) + [("sg", si)], writes=[("A:h1T", c, blk)])
        nh = None
        if grp == 0:
            nh = norm_act(C, C.X[:, 8, :], ("X", 8), 8)
        for i, t in enumerate(tiles):
            if grp == 0:
                norm_pe(C, nh, gcol, gkey, hT[:, :, i * 128:(i + 1) * 128], ("A:hT", i))
                if i + 1 < 8:
                    nh = norm_act(C, C.X[:, 8 + i + 1, :], ("X", 8 + i + 1), 8 + i + 1)
            b = next_ps(C, 2)
            blk = i // 4
            for hf in range(2):
                for c in range(22):
                    P.pe(lambda e, c=c, hf=hf, b=b, i=i: e.matmul(C.PS[:, b + hf, :], lhsT=h1T[:, c, i * 128:(i + 1) * 128], rhs=w2s[:, c, hf * 512:(hf + 1) * 512], start=(c == 0), stop=(c == 21)),
                         reads=[("A:h1T", c, blk), ("A:w2", c // 11)], writes=psk(b + hf))
            post_norm_update(C, b, t, gpost, gpkey)


def fm_proj(C, src_chunk_ap, hT, hkeys, n, bias_ap, func, dst, dst_key, extra_reads=(), scale=None, psview=None, pkey=None):
    P = C.P
    a, k = load_chunk(C, src_chunk_ap, "fm", pkey=pkey)
    kcn = src_chunk_ap.shape[1]
    b = next_ps(C)
    for kc in range(kcn):
        P.pe(lambda e, kc=kc: e.matmul(C.PS[:, b, 0:n], lhsT=a[:, kc, :], rhs=hT[:, kc, 0:n], start=(kc == 0), stop=(kc == kcn - 1)),
             reads=[k] + list(hkeys), writes=psk(b))
    kw = {}
    if bias_ap is not None:
        kw["bias"] = bias_ap
    if scale is not None:
        kw["scale"] = scale
    src = C.PS[:, b, 0:n] if psview is None else psview(C.PS[:, b, 0:n])
    P.act(lambda e: e.activation(out=dst, in_=src, func=func, **kw), reads=psk(b) + [("A:binfm", 0)] + list(extra_reads), writes=[dst_key])


def mixer_phase(C, l):
    P, dram = C.P, C.dram
    pre = "L%d_" % l
    C.ps_lo, C.ps_hi = 0, 8
    load_chunk(C, dram[pre + "win_fm"][CI_KCMP], "fm", pkey=("m1", l, CI_KCMP), prefetch=True)
    load_chunk(C, dram[pre + "win_fm"][CI_VCMP], "fm", pkey=("m1", l, CI_VCMP), prefetch=True)
    P.barrier(lambda e: e.memset(C.bar[:], 0.0))
    off = [0]

    def A(shape, dt):
        ap = carve(C, off[0], shape, dt)
        nb = int(np.prod(shape[1:])) * (2 if dt == BF16 else 4)
        off[0] += (nb + 31) // 32 * 32
        return ap

    kslc = A([128, 2, S], BF16)
    kwin = A([128, 2, S], BF16)
    Vaug = A([128, NT, 4, 65], BF16)
    kc2 = A([128, 2, 128], BF16)
    Vc = A([128, 2, 98], BF16)
    binfm = A([128, N_FM], F32)
    btm = A([128, N_TM], BF16)
    hTp = A([128, 8, 512], BF16)
    halo = A([128, 8, 16], BF16)
    nsa_oT = A([128, 4, 512], BF16)
    wmo = A([128, 8, 1024], BF16)
    base = off[0]
    fm = dram[pre + "win_fm"]
    wtm = dram[pre + "win_tm"]
    P.dma("sp", lambda e: e.dma_start(out=binfm, in_=dram[pre + "bin_fm"]), writes=[("A:binfm", 0)])
    P.dve(lambda e: e.memset(btm, 0.0), writes=[("A:btm", 0)])
    P.dma("pool", lambda e: e.dma_start(out=btm[0:1, :], in_=dram[pre + "bin_tm"]), writes=[("A:btm", 0)])
    gcol, gkey, gpost, gpkey = load_gains(C, dram[pre + "mix_pre"], dram[pre + "mix_post"], 1.0)
    P.dve(lambda e: e.memset(Vaug[:, :, :, 64:65], 1.0), writes=[("A:Vones", 0)])
    P.dve(lambda e: e.memset(Vc[:, :, 64:65], 1.0), writes=[("A:Vcones", 0)])
    for g in range(2):
        P.dma("pool", lambda e, g=g: e.dma_start(out=Vc[0:127, g, 65:97], in_=dram["c_ov"]), writes=[("A:Vcov", g)])

    rms_stats(C, [C.X[:, t, :] for t in range(NT)], [("X", t) for t in range(NT)], 0)

    def do_norm(blk, par):
        tiles = list(range(blk * 4, blk * 4 + 4))
        hT = hTb[par]
        hkeys = [("A:hT", par, i) for i in range(4)]
        for i, t in enumerate(tiles):
            norm_transpose(C, C.X[:, t, :], ("X", t), blk * 4 + i, gcol, gkey, hT[:, :, i * 128:(i + 1) * 128], hkeys[i])
        return hT, hkeys

    off[0] = base
    hTb = [hTp, A([128, 8, 512], BF16)]
    kcmpT = A([128, S], BF16)
    vcmpT = A([128, S], BF16)
    wtv = A([128, 8, 256], BF16)
    cw1 = A([128, 32, 256], BF16)
    posT = A([128, 32], BF16)
    cw2k = A([128, 2, 128], BF16)
    cw2v = A([128, 2, 64], BF16)
    hcmp = A([128, 2, 2, 128], BF16)
    P.dma("pool", lambda e: e.dma_start(out=wtv, in_=wtm[:, :, 0:256]), writes=[("A:wtv", 0)])
    for blk in range(4):
        hT, hkeys = do_norm(blk, blk % 2)
        if blk == 1:
            for hf in range(2):
                P.dma("pool", lambda e, hf=hf: e.dma_start(out=wmo[:, hf * 4:(hf + 1) * 4, :], in_=dram[pre + "w_mo"][:, hf * 4:(hf + 1) * 4, :]), writes=[("A:wmo", hf)])
        cs = slice(blk * 512, (blk + 1) * 512)
        de = lambda T: T.rearrange("p (r q) -> p r q", q=128)[:, :, blk * 32:(blk + 1) * 32].rearrange("p r q -> p q r")
        pv3 = lambda ps: ps.rearrange("p (q r) -> p q r", r=16)
        fm_proj(C, fm[CI_KCMP], hT, hkeys, 512, binfm[:, CI_KCMP:CI_KCMP + 1], AF.Identity, de(kcmpT), ("A:kcmpT", blk), psview=pv3, pkey=("m1", l, CI_KCMP) if blk == 0 else None)
        fm_proj(C, fm[CI_VCMP], hT, hkeys, 512, binfm[:, CI_VCMP:CI_VCMP + 1], AF.Identity, de(vcmpT), ("A:vcmpT", blk), psview=pv3, pkey=("m1", l, CI_VCMP) if blk == 0 else None)
        for g in range(2):
            fm_proj(C, fm[CI_KSLC + g], hT, hkeys, 512, binfm[:, CI_KSLC + g:CI_KSLC + g + 1], AF.Identity, kslc[:, g, cs], ("A:kslc", g, blk))
            fm_proj(C, fm[CI_KWIN + g], hT, hkeys, 512, binfm[:, CI_KWIN + g:CI_KWIN + g + 1], AF.Identity, kwin[:, g, cs], ("A:kwin", g, blk))
        for i in range(4):
            t = blk * 4 + i
            b = next_ps(C)
            for kc in range(8):
                P.pe(lambda e, kc=kc, b=b, i=i, hT=hT: e.matmul(C.PS[:, b, 0:256], lhsT=hT[:, kc, i * 128:(i + 1) * 128], rhs=wtv[:, kc, :], start=(kc == 0), stop=False),
                     reads=[hkeys[i], ("A:wtv", 0)], writes=psk(b))
            P.pe(lambda e, b=b: e.matmul(C.PS[:, b, 0:256], lhsT=C.e0[:, :], rhs=btm[:, 0:256], start=False, stop=True),
                 reads=["e0", ("A:btm", 0)], writes=psk(b))
            P.dve(lambda e, b=b, t=t: e.tensor_copy(out=Vaug[:, t, :, 0:64], in_=C.PS[:, b, 0:256].rearrange("p (a d) -> p a d", d=64)),
                  reads=psk(b), writes=[("A:Vaug", t)])
    for ki, kv in enumerate(("k", "v")):
        srcT = kcmpT if kv == "k" else vcmpT
        skeys = [("A:%scmpT" % kv, blk) for blk in range(4)]
        for hf in range(2):
            P.dma("pool", lambda e, hf=hf, kv=kv: e.dma_start(out=cw1[:, hf * 16:(hf + 1) * 16, :], in_=dram[pre + "cw1_" + kv][:, hf * 16:(hf + 1) * 16, :]),
                  writes=[("A:cw1", hf)])
        P.dma("pool", lambda e, kv=kv: e.dma_start(out=posT, in_=dram[pre + "cpos_" + kv]), writes=[("A:posT", 0)])
        if kv == "k":
            P.dma("pool", lambda e: e.dma_start(out=cw2k, in_=dram[pre + "cw2_k"]), writes=[("A:cw2k", 0)])
        else:
            P.dma("pool", lambda e: e.dma_start(out=cw2v, in_=dram[pre + "cw2_v"]), writes=[("A:cw2v", 0)])
        for hc in range(2):
            b = next_ps(C)
            for lpos in range(32):
                P.pe(lambda e, lpos=lpos, b=b, hc=hc: e.matmul(C.PS[:, b, 0:1], lhsT=cw1[0:64, lpos, hc * 128:(hc + 1) * 128], rhs=posT[0:64, lpos:lpos + 1], start=(lpos == 0), stop=(lpos == 31)),
                     reads=[("A:cw1", lpos // 16), ("A:posT", 0)], writes=psk(b))
            P.dve(lambda e, b=b, hc=hc: e.tensor_copy(out=C.small[:, 8 + hc:9 + hc], in_=C.PS[:, b, 0:1]), reads=psk(b), writes=[("cb", hc)])
        for g in range(2):
            rows = slice(g * 64, g * 64 + 64)
            for hc in range(2):
                b = next_ps(C)
                for lpos in range(32):
                    P.pe(lambda e, lpos=lpos, b=b, hc=hc, rows=rows, srcT=srcT: e.matmul(C.PS[:, b, 0:127], lhsT=cw1[rows, lpos, hc * 128:(hc + 1) * 128], rhs=srcT[rows, (lpos % 16) * 128 + lpos // 16:(lpos % 16) * 128 + lpos // 16 + 127], start=(lpos == 0), stop=(lpos == 31)),
                         reads=[("A:cw1", lpos // 16)] + skeys, writes=psk(b))
                P.act(lambda e, b=b, hc=hc, g=g: e.activation(out=hcmp[:, hc, g, 0:127], in_=C.PS[:, b, 0:127], func=AF.Gelu_apprx_tanh, bias=C.small[:, 8 + hc:9 + hc]),
                      reads=psk(b) + [("cb", hc)], writes=[("A:hcmp", hc, g)])
            b = next_ps(C)
            if kv == "k":
                for hc in range(2):
                    P.pe(lambda e, b=b, hc=hc, g=g: e.matmul(C.PS[:, b, 0:127], lhsT=cw2k[:, hc, :], rhs=hcmp[:, hc, g, 0:127], start=(hc == 0), stop=(hc == 1)),
                         reads=[("A:cw2k", 0), ("A:hcmp", hc, g)], writes=psk(b))
                P.dve(lambda e, b=b, g=g: e.tensor_copy(out=kc2[:, g, 0:127], in_=C.PS[:, b, 0:127]), reads=psk(b), writes=[("A:kc2", g)])
            else:
                for hc in range(2):
                    P.pe(lambda e, b=b, hc=hc, g=g: e.matmul(C.PS[0:127, b, 0:64], lhsT=hcmp[:, hc, g, 0:127], rhs=cw2v[:, hc, :], start=(hc == 0), stop=(hc == 1)),
                         reads=[("A:cw2v", 0), ("A:hcmp", hc, g)], writes=psk(b))
                P.dve(lambda e, b=b, g=g: e.tensor_copy(out=Vc[0:127, g, 0:64], in_=C.PS[0:127, b, 0:64]), reads=psk(b), writes=[("A:Vc", g)])

    for blk in range(4):
        cs = slice(blk * 512, (blk + 1) * 512)
        for c in range(2):
            load_chunk(C, fm[CI_Q + c], "q", pkey=("q", l, blk, c), prefetch=True)
        P.barrier(lambda e: e.memset(C.bar[:], 0.0))
        C.ps_lo, C.ps_hi = 0, 4
        off[0] = base
        cmpmask = A([128, 512], BF16)
        eall = A([128, S], BF16)
        causneg = A([128, 4, 512], BF16)
        winneg = A([128, 6, 512], BF16)
        qP = A([128, 8, 512], BF16)
        Er = [A([128, 512], BF16) for _ in range(4)]
        oacc = A([128, 4, 512], F32)
        negmT = A([128, 2, 512], BF16)
        gates = A([128, 4, 24], F32)
        wg = A([128, 8, 24], BF16)
        vmask = A([128, 4, 32], F32)
        addt = A([128, 4, 32], F32)
        imp = A([128, 4, 32], F32)
        prod = A([128, 4, 4, 32], F32)
        m8 = A([128, 4, 8], F32)
        negm2 = [A([128, 4, 128], BF16) for _ in range(2)]
        fac = A([128, 16], F32)
        rinv = A([128, 16], F32)
        tmpo = A([128, 2, 256], F32)
        otok = A([128, 512], BF16)
        accsb = A([128, 4, 388], F32)
        if blk > 0:
            P.act(lambda e: e.copy(out=halo, in_=hTp[:, :, 496:512]), reads=[("A:hT", 0, 3)], writes=[("A:halo", 0)])
        hT, hkeys = do_norm(blk, 0)
        cs = slice(blk * 512, (blk + 1) * 512)
        P.dma("sp", lambda e, cmpmask=cmpmask, cs=cs: e.dma_start(out=cmpmask[0:127, :], in_=dram["c_cmpmask"][:, cs]), writes=[("A:cmpmask", 0)])
        P.dve(lambda e, eall=eall: e.memset(eall, 0.0), writes=[("A:eall", 0)])
        P.dma("sp", lambda e, eall=eall: e.dma_start(out=eall[0:32, :], in_=dram["c_eall"]), writes=[("A:eall", 0)])
        P.dma("sp", lambda e, causneg=causneg: e.dma_start(out=causneg, in_=dram["c_causneg"]), writes=[("A:causneg", 0)])
        P.dma("sp", lambda e, winneg=winneg: e.dma_start(out=winneg, in_=dram["c_winneg"]), writes=[("A:winneg", 0)])
        P.dve(lambda e, qP=qP: e.memset(qP, 0.0), writes=[("A:qP", h) for h in range(8)])
        P.dve(lambda e, negmT=negmT: e.memset(negmT, 0.0), writes=[("A:negmT", g) for g in range(2)])
        for g_ in range(2):
            P.dve(lambda e, g_=g_: e.memset(negm2[g_], 0.0), writes=[("A:negm", g_)])
        P.dma("pool", lambda e, wg=wg: e.dma_start(out=wg, in_=wtm[:, :, TM_G:TM_G + 24]), writes=[("A:wg", 0)])
        P.dma("sp", lambda e, vmask=vmask, blk=blk: e.dma_start(out=vmask, in_=dram["c_vmask"][:, blk * 4:(blk + 1) * 4, :]), writes=[("A:vmask", 0)])
        P.dma("sp", lambda e, addt=addt, blk=blk: e.dma_start(out=addt, in_=dram["c_addterm"][:, blk * 4:(blk + 1) * 4, :]), writes=[("A:addt", 0)])
        for c in range(4):
            a, k = load_chunk(C, fm[CI_Q + c], "q", pkey=("q", l, blk, c))
            b = next_ps(C)
            for kc in range(8):
                P.pe(lambda e, kc=kc, b=b, a=a, hT=hT: e.matmul(C.PS[:, b, :], lhsT=a[:, kc, :], rhs=hT[:, kc, :], start=(kc == 0), stop=(kc == 7)),
                     reads=[k] + hkeys, writes=psk(b))
            for half in range(2):
                rs = slice(half * 64, half * 64 + 64)
                P.act(lambda e, b=b, rs=rs, c=c, half=half, qP=qP: e.activation(out=qP[rs, 2 * c + half, :], in_=C.PS[rs, b, :], func=AF.Identity, bias=binfm[rs, CI_Q + c:CI_Q + c + 1]),
                      reads=psk(b) + [("A:binfm", 0)], writes=[("A:qP", 2 * c + half)])
        for i in range(4):
            b = next_ps(C)
            for kc in range(8):
                P.pe(lambda e, kc=kc, b=b, i=i, hT=hT, wg=wg: e.matmul(C.PS[:, b, 0:24], lhsT=hT[:, kc, i * 128:(i + 1) * 128], rhs=wg[:, kc, :], start=(kc == 0), stop=False),
                     reads=[hkeys[i], ("A:wg", 0)], writes=psk(b))
            P.pe(lambda e, b=b: e.matmul(C.PS[:, b, 0:24], lhsT=C.e0[:, :], rhs=btm[:, TM_G:TM_G + 24], start=False, stop=True),
                 reads=["e0", ("A:btm", 0)], writes=psk(b))
            P.act(lambda e, b=b, i=i, gates=gates: e.activation(out=gates[:, i, :], in_=C.PS[:, b, 0:24], func=AF.Sigmoid), reads=psk(b), writes=[("A:gates", i)])
        gkeys = [("A:gates", i) for i in range(4)]
        e_i = [0]
        groups = [(0, 0), (0, 1), (2, 0), (2, 1), (1, 0), (1, 1)]
        seq = []
        for gidx, (br, g) in enumerate(groups):
            if br == 0:
                kts = [0]
            elif br == 1:
                kts = list(range(0, 4 * blk + 4))
            else:
                kts = list(range(max(0, 4 * blk - 2), 4 * blk + 4))
            tl = [(gidx, hh, kt) for hh in range(4) for kt in kts]
            for n_, t_ in enumerate(tl):
                seq.append(t_ + (n_ == 0, n_ == len(tl) - 1))
        tinfo = {}
        deferred = []

        def run_deferred(force=False):
            for d_ in list(deferred):
                d_[0] -= 1
                if force or d_[0] <= 0:
                    deferred.remove(d_)
                    d_[1]()

        def front(tile, blk=blk, cs=cs, hT=hT):
            gidx, hh, kt, first, last = tile
            br, g = groups[gidx]
            h = 4 * g + hh
            if br == 1 and first:
                run_deferred(force=True)
            bS = next_ps(C)
            ks = slice(kt * 128, (kt + 1) * 128)
            if br == 0:
                P.pe(lambda e: e.matmul(C.PS[0:127, bS, :], lhsT=kc2[:, g, 0:127], rhs=qP[:, h, :], start=True, stop=True),
                     reads=[("A:qP", h)], writes=psk(bS))
            elif br == 1:
                diag = kt >= 4 * blk
                P.pe(lambda e: e.matmul(C.PS[:, bS, :], lhsT=kslc[:, g, ks], rhs=qP[:, h, :], start=True, stop=False),
                     reads=[("A:qP", h)], writes=psk(bS))
                P.pe(lambda e: e.matmul(C.PS[:, bS, :], lhsT=eall[:, ks], rhs=negmT[:, g, :], start=False, stop=(not diag)),
                     reads=[("A:negmT", g), ("A:eall", 0)], writes=psk(bS))
                if diag:
                    P.pe(lambda e: e.matmul(C.PS[:, bS, :], lhsT=C.ident[:], rhs=causneg[:, kt - 4 * blk, :], start=False, stop=True),
                         reads=["ident", ("A:causneg", 0)], writes=psk(bS))
            else:
                P.pe(lambda e: e.matmul(C.PS[:, bS, :], lhsT=kwin[:, g, ks], rhs=qP[:, h, :], start=True, stop=False),
                     reads=[("A:qP", h)], writes=psk(bS))
                P.pe(lambda e: e.matmul(C.PS[:, bS, :], lhsT=C.ident[:], rhs=winneg[:, kt - 4 * blk + 2, :], start=False, stop=True),
                     reads=["ident", ("A:winneg", 0)], writes=psk(bS))
            si = e_i[0]
            e_i[0] = (si + 1) % 4
            Et = Er[si]
            np_ = 127 if br == 0 else 128
            P.act(lambda e: e.activation(out=Et[0:np_, :], in_=C.PS[0:np_, bS, :], func=AF.Exp, scale=0.125),
                  reads=psk(bS), writes=[("A:E", si)])
            if br == 0:
                P.dve(lambda e: e.tensor_tensor(out=Et[0:127, :], in0=Et[0:127, :], in1=cmpmask[0:127, :], op=ALU.mult),
                      reads=[("A:E", si), ("A:cmpmask", 0)], writes=[("A:E", si)])
            tinfo[tile] = (si, Et, np_)

        def back(tile, blk=blk, cs=cs):
            gidx, hh, kt, first, last = tile
            br, g = groups[gidx]
            W = 97 if br == 0 else 65
            si, Et, np_ = tinfo[tile]
            if first:
                for i in range(4):
                    P.pe(lambda e, i=i: e.matmul(C.PS[:, 4 + i, :], lhsT=C.zeros[:, 0:128], rhs=C.zeros[:, :], start=True, stop=True),
                         reads=["zeros"], writes=psk(4 + i))
            for i in range(4):
                qt = 4 * blk + i
                if br == 1 and kt > qt:
                    continue
                if br == 2 and not (qt - 2 <= kt <= qt):
                    continue
                if br == 0:
                    rhs = Vc[0:127, g, 0:97]
                elif br == 1:
                    rhs = Vaug[:, kt, g, :]
                else:
                    rhs = Vaug[:, kt, 2 + g, :]
                P.pe(lambda e, i=i, rhs=rhs: e.matmul(C.PS[:, 4 + i, hh * W:(hh + 1) * W], lhsT=Et[0:np_, i * 128:(i + 1) * 128], rhs=rhs, start=False, stop=False, skip_group_check=True),
                     reads=[("A:E", si)], writes=psk(4 + i))
            run_deferred()
            if last:
                post(br, g, W)

        def post(br, g, W):
            P.dve(lambda e: e.tensor_copy(out=accsb[:, :, 0:4 * W], in_=C.PS[:, 4:8, 0:4 * W]), reads=psk(4, 4), writes=[("A:accsb", 0)])
            accs = accsb[:, :, 0:4 * W].rearrange("p t (h w) -> p t h w", w=W)
            ak = [("A:accsb", 0)]
            r4 = rinv.rearrange("p (t h o) -> p t h o", h=4, o=1)
            P.dve(lambda e: e.tensor_scalar(out=r4, in0=accs[:, :, :, 64:65], scalar1=1e-30, scalar2=None, op0=ALU.max),
                  reads=ak, writes=[("A:rinv", 0)])
            P.dve(lambda e: e.reciprocal(out=rinv, in_=rinv), reads=[("A:rinv", 0)], writes=[("A:rinv", 0)])
            if br == 0:
                P.dve(lambda e: e.tensor_tensor(out=prod, in0=accs[:, :, :, 65:97], in1=r4.broadcast_to([128, 4, 4, 32]), op=ALU.mult),
                      reads=ak + [("A:rinv", 0)], writes=[("A:prod", 0)])
                P.dve(lambda e: e.tensor_tensor(out=imp, in0=prod[:, :, 0, :], in1=prod[:, :, 1, :], op=ALU.add), reads=[("A:prod", 0)], writes=[("A:imp", 0)])
                P.dve(lambda e: e.tensor_tensor(out=imp, in0=imp, in1=prod[:, :, 2, :], op=ALU.add), reads=[("A:prod", 0), ("A:imp", 0)], writes=[("A:imp", 0)])
                P.dve(lambda e: e.tensor_tensor(out=imp, in0=imp, in1=prod[:, :, 3, :], op=ALU.add), reads=[("A:prod", 0), ("A:imp", 0)], writes=[("A:imp", 0)])
                P.dve(lambda e: e.tensor_tensor(out=imp, in0=imp, in1=vmask, op=ALU.mult), reads=[("A:imp", 0), ("A:vmask", 0)], writes=[("A:imp", 0)])
                P.dve(lambda e: e.tensor_tensor(out=imp, in0=imp, in1=addt, op=ALU.add), reads=[("A:imp", 0), ("A:addt", 0)], writes=[("A:imp", 0)])
                for i in range(4):
                    P.dve(lambda e, i=i: e.max(out=m8[:, i, :], in_=imp[:, i, :]), reads=[("A:imp", 0)], writes=[("A:m8", i)])
                P.dve(lambda e: e.tensor_tensor(out=imp, in0=imp, in1=m8[:, :, 7:8].broadcast_to([128, 4, 32]), op=ALU.is_ge),
                      reads=[("A:imp", 0)] + [("A:m8", i) for i in range(4)], writes=[("A:imp", 0)])
                P.dve(lambda e: e.tensor_scalar(out=negm2[g][:, :, 0:32], in0=imp, scalar1=-NEGM, scalar2=NEGM, op0=ALU.mult, op1=ALU.add),
                      reads=[("A:imp", 0)], writes=[("A:negm", g)])
                def tr_sel(g=g):
                    bT = next_ps(C)
                    pv = C.PS[:, bT, :].bitcast(BF16)
                    for i in range(4):
                        P.pe(lambda e, i=i: e.transpose(out=pv[:, i * 128:(i + 1) * 128], in_=negm2[g][:, i, :], identity=C.ident[:]), reads=[("A:negm", g), "ident"], writes=psk(bT))
                    P.act(lambda e: e.copy(out=negmT[0:32, g, :], in_=pv[0:32, 0:512]), reads=psk(bT), writes=[("A:negmT", g)])
                deferred.append([4, tr_sel])
            gv = gates.rearrange("p t (h r) -> p t h r", r=3)[:, :, 4 * g:4 * g + 4, br:br + 1]
            f4 = fac.rearrange("p (t h o) -> p t h o", h=4, o=1)
            P.dve(lambda e: e.tensor_tensor(out=f4, in0=r4, in1=gv, op=ALU.mult), reads=[("A:rinv", 0)] + gkeys, writes=[("A:fac", 0)])
            for hp in range(2):
                ts = slice(hp * 2, hp * 2 + 2)
                od = oacc[:, ts, g * 256:(g + 1) * 256].rearrange("p t (h d) -> p t h d", d=64)
                fb = f4[:, ts].broadcast_to([128, 2, 4, 64])
                okeys = [("A:oacc", hp, g)]
                if br == 0:
                    P.dve(lambda e, od=od, fb=fb, ts=ts: e.tensor_tensor(out=od, in0=accs[:, ts, :, 0:64], in1=fb, op=ALU.mult), reads=ak + [("A:fac", 0)], writes=okeys)
                else:
                    tv = tmpo[:, hp, :].rearrange("p (t h d) -> p t h d", t=2, d=64) if False else tmpo.rearrange("p t (h d) -> p t h d", d=64)
                    P.dve(lambda e, tv=tv, fb=fb, ts=ts: e.tensor_tensor(out=tv, in0=accs[:, ts, :, 0:64], in1=fb, op=ALU.mult), reads=ak + [("A:fac", 0)], writes=[("A:tmpo", 0)])
                    P.dve(lambda e, od=od, tv=tv: e.tensor_tensor(out=od, in0=od, in1=tv, op=ALU.add), reads=[("A:tmpo", 0)] + okeys, writes=okeys)

        SKEW = 2
        for n_ in range(len(seq) + SKEW):
            if n_ < len(seq):
                front(seq[n_])
            if n_ >= SKEW:
                back(seq[n_ - SKEW])
        for i in range(4):
            P.act(lambda e, i=i, otok=otok, oacc=oacc: e.copy(out=otok, in_=oacc[:, i, :]), reads=[("A:oacc", i // 2, 0), ("A:oacc", i // 2, 1)], writes=[("A:otok", 0)])
            bT = next_ps(C)
            pv = C.PS[:, bT, :].bitcast(BF16)
            for c in range(4):
                P.pe(lambda e, pv=pv, c=c, otok=otok: e.transpose(out=pv[:, c * 128:(c + 1) * 128], in_=otok[:, c * 128:(c + 1) * 128], identity=C.ident[:]),
                     reads=[("A:otok", 0), "ident"], writes=psk(bT))
            P.dve(lambda e, pv=pv, i=i: e.tensor_copy(out=nsa_oT[:, :, i * 128:(i + 1) * 128], in_=pv[:, 0:512].rearrange("p (c t) -> p c t", t=128)),
                  reads=psk(bT), writes=[("A:nsa_oT", i)])

        load_chunk(C, fm[CI_POOL], "pool", pkey=("pool", l, blk, 0), prefetch=True)
        load_chunk(C, fm[CI_U], "u", pkey=("u", l, blk, 0), prefetch=True)
        load_chunk(C, fm[CI_U + 1], "u", pkey=("u", l, blk, 1), prefetch=True)
        P.barrier(lambda e: e.memset(C.bar[:], 0.0))
        C.ps_lo, C.ps_hi = 0, 8
        off[0] = base
        abuf = [A([128, 528], F32) for _ in range(3)]
        pooled = A([128, 512], BF16)
        poolmix = A([128, 4, 512], BF16)
        pw = A([128, 4, 128], BF16)
        pscale = A([128, 4], F32)
        pfix = A([128, 4, 16], F32)
        uT = A([128, 4, 512], BF16)
        gvf = [A([128, 512], F32) for _ in range(2)]
        vn = [A([128, 512], BF16) for _ in range(2)]
        gws = A([128, 4, 128], BF16)
        tril = A([128, 128], BF16)
        lng = A([128, 512], F32)
        lnb = A([128, 512], F32)
        bsb = A([128, 512], F32)
        wgv = A([128, 8, 512], BF16)
        merged = A([128, 8, 512], BF16)
        gsig = [[A([128, 512], BF16) for _ in range(3)] for _ in range(2)]
        tA = [A([128, 512], F32) for _ in range(2)]
        bst = A([128, 2, 8], F32)
        assert off[0] <= C.arena_bytes, off[0]
        C.m2b_used = off[0]
        hT, hkeys = hTp, [("A:hT", 0, i) for i in range(4)]
        nsak = [("A:nsa_oT", i) for i in range(4)]
        P.dma("pool", lambda e, pw=pw: e.dma_start(out=pw, in_=dram[pre + "pool_w"]), writes=[("A:pw", 0)])
        P.dma("sp", lambda e, pscale=pscale: e.dma_start(out=pscale, in_=dram[pre + "pool_scale"]), writes=[("A:pscale", 0)])
        P.dma("sp", lambda e, pfix=pfix: e.dma_start(out=pfix, in_=dram["c_poolfix"]), writes=[("A:pfix", 0)])
        P.dma("sp", lambda e, lng=lng: e.dma_start(out=lng, in_=dram[pre + "ln_g"].partition_broadcast(128)), writes=[("A:lng", 0)])
        P.dma("sp", lambda e, lnb=lnb: e.dma_start(out=lnb, in_=dram[pre + "ln_b"].partition_broadcast(128)), writes=[("A:lnb", 0)])
        P.dma("sp", lambda e, bsb=bsb: e.dma_start(out=bsb, in_=dram[pre + "gbs"].partition_broadcast(128)), writes=[("A:bsb", 0)])

        def late_loads(gws=gws, tril=tril, wgv=wgv):
            P.dma("pool", lambda e: e.dma_start(out=gws, in_=dram[pre + "gws"]), writes=[("A:gws", 0)])
            P.dma("pool", lambda e: e.dma_start(out=tril, in_=dram["c_trilT"]), writes=[("A:tril", 0)])
            P.dve(lambda e: e.tensor_tensor(out=gws, in0=gws, in1=tril.unsqueeze(1).broadcast_to([128, 4, 128]), op=ALU.mult),
                  reads=[("A:gws", 0), ("A:tril", 0)], writes=[("A:gws", 0)])
            P.dma("pool", lambda e: e.dma_start(out=wgv, in_=wtm[:, :, TM_GV:TM_GV + 512]), writes=[("A:wgv", 0)])

        A0, A1, A2 = abuf

        def pool_front(gi, blk=blk, hT=hT, hkeys=hkeys):
            a, k = load_chunk(C, fm[CI_POOL + gi], "pool", pkey=("pool", l, blk, gi))
            b = next_ps(C)
            for kc in range(8):
                P.pe(lambda e, kc=kc: e.matmul(C.PS[:, b, :], lhsT=a[:, kc, :], rhs=hT[:, kc, :], start=(kc == 0), stop=(kc == 7)),
                     reads=[k] + hkeys, writes=psk(b))
            bcol = binfm[:, CI_POOL + gi:CI_POOL + gi + 1]
            if blk > 0:
                b2 = next_ps(C)
                for kc in range(8):
                    P.pe(lambda e, kc=kc: e.matmul(C.PS[:, b2, 0:16], lhsT=a[:, kc, :], rhs=halo[:, kc, :], start=(kc == 0), stop=(kc == 7)),
                         reads=[k, ("A:halo", 0)], writes=psk(b2))
            P.act(lambda e: e.activation(out=A0[:, 16:528], in_=C.PS[:, b, :], func=AF.Identity, bias=bcol), reads=psk(b) + [("A:binfm", 0)], writes=[("A:a0", 1)])
            if blk == 0:
                P.dve(lambda e: e.memset(A0[:, 0:16], 0.0), writes=[("A:a0", 0)])
            else:
                P.act(lambda e: e.activation(out=A0[:, 0:16], in_=C.PS[:, b2, 0:16], func=AF.Identity, bias=bcol), reads=psk(b2) + [("A:binfm", 0)], writes=[("A:a0", 0)])
            cur, ckey = A0, [("A:a0", 0), ("A:a0", 1)]
            for si, stp in enumerate([1, 2, 4, 8][:gi + 1]):
                nxt = A1 if si % 2 == 0 else A2
                nkey = ("A:a%d" % (1 if si % 2 == 0 else 2), 0)
                P.dve(lambda e, cur=cur, nxt=nxt, stp=stp: e.tensor_tensor(out=nxt[:, stp:528], in0=cur[:, stp:528], in1=cur[:, 0:528 - stp], op=ALU.add),
                      reads=ckey, writes=[nkey])
                cur, ckey = nxt, [nkey]
            if blk == 0:
                P.dve(lambda e, cur=cur: e.tensor_tensor(out=cur[:, 16:32], in0=cur[:, 16:32], in1=pfix[:, gi, :], op=ALU.mult), reads=ckey + [("A:pfix", 0)], writes=ckey)
            P.dve(lambda e, cur=cur: e.scalar_tensor_tensor(out=pooled, in0=cur[:, 16:528], scalar=1.0 / (2 << gi), in1=A0[:, 16:528], op0=ALU.mult, op1=ALU.subtract),
                  reads=ckey + [("A:a0", 1)], writes=[("A:pooled", 0)])

        def pool_back(gi):
            b = next_ps(C)
            P.pe(lambda e: e.matmul(C.PS[:, b, :], lhsT=pw[:, gi, :], rhs=pooled, start=True, stop=True), reads=[("A:pw", 0), ("A:pooled", 0)], writes=psk(b))
            P.act(lambda e: e.activation(out=poolmix[:, gi, :], in_=C.PS[:, b, :], func=AF.Identity, scale=pscale[:, gi:gi + 1]),
                  reads=psk(b) + [("A:pscale", 0)], writes=[("A:poolmix", gi)])

        def gate_proj(brn, j, hT=hT, hkeys=hkeys):
            ci = CI_MG + brn * 8 + j
            fm_proj(C, fm[ci], hT, hkeys, 512, binfm[:, ci:ci + 1], AF.Sigmoid, gsig[j % 2][brn], ("A:gsig", j % 2, brn))

        def gm_front_pair(i0_, hT=hT, hkeys=hkeys):
            bs_ = []
            for i in (i0_, i0_ + 1):
                d = i % 2
                b = next_ps(C)
                for kc in range(8):
                    P.pe(lambda e, kc=kc, b=b, i=i: e.matmul(C.PS[:, b, :], lhsT=hT[:, kc, i * 128:(i + 1) * 128], rhs=wgv[:, kc, :], start=(kc == 0), stop=False),
                         reads=[hkeys[i], ("A:wgv", 0)], writes=psk(b))
                P.pe(lambda e, b=b: e.matmul(C.PS[:, b, :], lhsT=C.e0[:, :], rhs=btm[:, TM_GV:TM_GV + 512], start=False, stop=True), reads=["e0", ("A:btm", 0)], writes=psk(b))
                bs_.append(b)
            for i, b in zip((i0_, i0_ + 1), bs_):
                d = i % 2
                P.act(lambda e, b=b, d=d: e.activation(out=gvf[d], in_=C.PS[:, b, :], func=AF.Gelu_apprx_tanh), reads=psk(b), writes=[("A:gvf", d)])
            for d in range(2):
                P.dve(lambda e, d=d: e.bn_stats(out=bst[:, d, 0:6], in_=gvf[d]), reads=[("A:gvf", d)], writes=[("A:bst", d)])
                P.dve(lambda e, d=d: e.bn_aggr(out=bst[:, d, 6:8], in_=bst[:, d, 0:6]), reads=[("A:bst", d)], writes=[("A:bst", d)])
            sk2 = [("A:bst", 0), ("A:bst", 1)]
            P.act(lambda e: e.activation(out=bst[:, :, 7:8], in_=bst[:, :, 7:8], func=AF.Sqrt, bias=C.epsb[:, 0:1]), reads=sk2 + ["epsb"], writes=sk2)
            P.dve(lambda e: e.reciprocal(out=bst[:, :, 7:8], in_=bst[:, :, 7:8]), reads=sk2, writes=sk2)
            for d in range(2):
                g_, v_, s_ = gvf[d], vn[d], bst[:, d, :]
                gk, vk, sk = ("A:gvf", d), ("A:vn", d), ("A:bst", d)
                P.dve(lambda e, g_=g_, s_=s_: e.tensor_scalar(out=g_, in0=g_, scalar1=s_[:, 6:7], scalar2=s_[:, 7:8], op0=ALU.subtract, op1=ALU.mult), reads=[gk, sk], writes=[gk])
                P.dve(lambda e, g_=g_: e.tensor_tensor(out=g_, in0=g_, in1=lng, op=ALU.mult), reads=[gk, ("A:lng", 0)], writes=[gk])
                P.dve(lambda e, g_=g_, v_=v_: e.tensor_tensor(out=v_, in0=g_, in1=lnb, op=ALU.add), reads=[gk, ("A:lnb", 0)], writes=[vk])

        def gm_back(i):
            d = i % 2
            b = next_ps(C)
            v_, t_ = vn[d], tA[d]
            for g in range(4):
                P.pe(lambda e, g=g: e.matmul(C.PS[:, b, g * 128:(g + 1) * 128], lhsT=v_[:, g * 128:(g + 1) * 128], rhs=gws[:, g, :], start=True, stop=True),
                     reads=[("A:vn", d), ("A:gws", 0)], writes=psk(b))
            P.dve(lambda e: e.tensor_tensor(out=t_, in0=C.PS[:, b, :], in1=bsb, op=ALU.add), reads=psk(b) + [("A:bsb", 0)], writes=[("A:tA", d)])
            uv = uT[:, :, i * 128:(i + 1) * 128]
            P.dve(lambda e: e.tensor_tensor(out=uv, in0=uv, in1=t_.rearrange("p (g t) -> p g t", t=128), op=ALU.mult),
                  reads=[("A:tA", d)] + [("A:uT", c) for c in range(4)], writes=[("A:gm", i)])

        def u_proj(c, hT=hT, hkeys=hkeys, blk=blk):
            fm_proj(C, fm[CI_U + c], hT, hkeys, 512, binfm[:, CI_U + c:CI_U + c + 1], AF.Gelu_apprx_tanh, uT[:, c, :], ("A:uT", c), pkey=("u", l, blk, c))

        pool_front(0)
        late_loads()
        u_proj(0)
        u_proj(1)
        pool_back(0)
        pool_front(1)
        gm_front_pair(0)
        pool_back(1)
        pool_front(2)
        u_proj(2)
        u_proj(3)
        gm_back(0)
        gm_back(1)
        pool_back(2)
        pool_front(3)
        for c in (0, 1, 2):
            gate_proj(c, 0)
        gm_front_pair(2)
        pool_back(3)
        for c in (0, 1, 2):
            gate_proj(c, 1)
        gm_back(2)
        gm_back(3)
        gmk = [("A:gm", i) for i in range(4)] + [("A:uT", c) for c in range(4)]
        branches = [(poolmix, [("A:poolmix", gi) for gi in range(4)]), (nsa_oT, nsak), (uT, gmk)]
        for j in range(8):
            tj = tA[0]
            tb = tA[1]
            for brn in range(3):
                src, skeys = branches[brn]
                a, k = load_chunk(C, dram[pre + "w_br"][brn * 8 + j], "wbr")
                b = next_ps(C)
                for kc in range(4):
                    P.pe(lambda e, kc=kc, b=b, a=a, src=src: e.matmul(C.PS[:, b, :], lhsT=a[:, kc, :], rhs=src[:, kc, :], start=(kc == 0), stop=(kc == 3)),
                         reads=[k] + skeys, writes=psk(b))
                gs = gsig[j % 2][brn]
                gk = ("A:gsig", j % 2, brn)
                if brn == 0:
                    P.dve(lambda e, b=b, gs=gs, tj=tj: e.tensor_tensor(out=tj, in0=C.PS[:, b, :], in1=gs, op=ALU.mult), reads=psk(b) + [gk], writes=[("A:tA", 0)])
                elif brn == 1:
                    P.dve(lambda e, b=b, gs=gs, tb=tb: e.tensor_tensor(out=tb, in0=C.PS[:, b, :], in1=gs, op=ALU.mult), reads=psk(b) + [gk], writes=[("A:tA", 1)])
                    P.dve(lambda e, tj=tj, tb=tb: e.tensor_tensor(out=tj, in0=tj, in1=tb, op=ALU.add), reads=[("A:tA", 0), ("A:tA", 1)], writes=[("A:tA", 0)])
                else:
                    P.dve(lambda e, b=b, gs=gs, tb=tb: e.tensor_tensor(out=tb, in0=C.PS[:, b, :], in1=gs, op=ALU.mult), reads=psk(b) + [gk], writes=[("A:tA", 1)])
                    P.dve(lambda e, tj=tj, tb=tb, j=j, merged=merged: e.tensor_tensor(out=merged[:, j, :], in0=tj, in1=tb, op=ALU.add), reads=[("A:tA", 0), ("A:tA", 1)], writes=[("A:merged", j)])
            if j + 2 < 8:
                for brn in range(3):
                    gate_proj(brn, j + 2)
        mk_ = [("A:merged", j) for j in range(8)]
        for i in range(4):
            b = next_ps(C, 2)
            for hf in range(2):
                for j in range(8):
                    P.pe(lambda e, j=j, hf=hf, b=b, i=i, merged=merged: e.matmul(C.PS[:, b + hf, :], lhsT=merged[:, j, i * 128:(i + 1) * 128], rhs=wmo[:, j, hf * 512:(hf + 1) * 512], start=(j == 0), stop=(j == 7)),
                         reads=mk_ + [("A:wmo", 0), ("A:wmo", 1)], writes=psk(b + hf))
            post_norm_update(C, b, blk * 4 + i, gpost, gpkey)


def cross_phase(C, l):
    P, dram = C.P, C.dram
    pre = "L%d_" % l
    C.ps_lo, C.ps_hi = 0, 4
    for c in range(2):
        load_chunk(C, dram[pre + "wk"][c], "wk", pkey=(pre, "wk", c), prefetch=True)
    P.barrier(lambda e: e.memset(C.bar[:], 0.0))
    wq = carve(C, 0, [128, 8, 8, 128], BF16)
    wo = carve(C, 16384, [128, 8, 1024], BF16)
    wv = carve(C, 32768, [128, 8, 1024], BF16)
    kT = carve(C, 49152, [128, 8, 256], BF16)
    Va = carve(C, 53248, [128, 2, 4, 257], BF16)
    memT = carve(C, 57376, [128, 8, 256], BF16)
    hTb = [carve(C, 61472 + i * 8192, [128, 8, 512], BF16) for i in range(2)]
    qT = carve(C, 77856, [128, 8, 512], BF16)
    E = carve(C, 86048, [128, 2, 4, 512], BF16)
    otok = carve(C, 94240, [128, 4, 256], BF16)
    oT = carve(C, 96288, [128, 8, 128], BF16)
    memf = carve(C, 98336, [128, 2, 1024], F32)
    wqd = dram[pre + "wq"].rearrange("c p k j -> p c k j")
    P.dma("sp", lambda e: e.dma_start(out=memf, in_=dram["mem"].rearrange("(t p) d -> p t d", p=128)), writes=[("A:memf", 0)])
    kcol, kkey, _, _ = load_gains(C, dram[pre + "mem_kvg"], dram[pre + "mem_post"], 1.0)
    gcol, gkey, gpost, gpkey = load_gains(C, dram[pre + "mem_pre"], dram[pre + "mem_post"], 1.0)
    P.dve(lambda e: e.memset(Va[:, :, :, 256:257], 1.0), writes=[("A:Vones", 0)])
    rms_stats(C, [memf[:, mt, :] for mt in range(2)], [("A:memf", 0)] * 2, 16)
    for mt in range(2):
        norm_transpose(C, memf[:, mt, :], ("A:memf", 0), 16 + mt, kcol, kkey, memT[:, :, mt * 128:(mt + 1) * 128], ("A:memT", mt))
    rms_stats(C, [C.X[:, t, :] for t in range(NT)], [("X", t) for t in range(NT)], 0)
    mk = [("A:memT", 0), ("A:memT", 1)]
    for c in range(8):
        a, k = load_chunk(C, dram[pre + "wk"][c], "wk", pkey=(pre, "wk", c))
        if c == 4:
            for hf in range(2):
                P.dma("pool", lambda e, hf=hf: e.dma_start(out=wv[:, hf * 4:(hf + 1) * 4, :], in_=dram[pre + "wv"][:, hf * 4:(hf + 1) * 4, :]), writes=[("A:wv", hf)])
        b = next_ps(C)
        for kc in range(8):
            P.pe(lambda e, kc=kc, a=a, b=b: e.matmul(C.PS[:, b, 0:256], lhsT=a[:, kc, :], rhs=memT[:, kc, :], start=(kc == 0), stop=(kc == 7)),
                 reads=[k] + mk, writes=psk(b))
        P.act(lambda e, b=b, c=c: e.copy(out=kT[:, c, :], in_=C.PS[:, b, 0:256]), reads=psk(b), writes=[("A:kT", c)])
    for hf in range(2):
        P.dma("pool", lambda e, hf=hf: e.dma_start(out=wq[:, hf * 4:(hf + 1) * 4], in_=wqd[:, hf * 4:(hf + 1) * 4]), writes=[("A:wq", hf)])
    for hf in range(2):
        P.dma("pool", lambda e, hf=hf: e.dma_start(out=wo[:, hf * 4:(hf + 1) * 4, :], in_=dram[pre + "wo"][:, hf * 4:(hf + 1) * 4, :]), writes=[("A:wo", hf)])
    for mt in range(2):
        for hf in range(2):
            b = next_ps(C)
            for kc in range(8):
                P.pe(lambda e, kc=kc, b=b, mt=mt, hf=hf: e.matmul(C.PS[:, b, :], lhsT=memT[:, kc, mt * 128:(mt + 1) * 128], rhs=wv[:, kc, hf * 512:(hf + 1) * 512], start=(kc == 0), stop=(kc == 7)),
                     reads=[("A:memT", mt), ("A:wv", 0), ("A:wv", 1)], writes=psk(b))
            P.dve(lambda e, b=b, mt=mt, hf=hf: e.tensor_copy(out=Va[:, mt, 2 * hf:2 * hf + 2, 0:256], in_=C.PS[:, b, :].rearrange("p (h d) -> p h d", d=256)),
                  reads=psk(b), writes=[("A:Va", mt, hf)])
    vkeys = [("A:Va", mt, hf) for mt in range(2) for hf in range(2)] + [("A:Vones", 0)]

    def do_norm(blk):
        hT = hTb[blk % 2]
        hkeys = [("A:hT", blk % 2, i) for i in range(4)]
        for i in range(4):
            t = blk * 4 + i
            norm_transpose(C, C.X[:, t, :], ("X", t), t, gcol, gkey, hT[:, :, i * 128:(i + 1) * 128], hkeys[i])
        return hT, hkeys

    def out_proj(t):
        b = next_ps(C, 2)
        for hf in range(2):
            for kc in range(8):
                P.pe(lambda e, kc=kc, hf=hf, b=b: e.matmul(C.PS[:, b + hf, :], lhsT=oT[:, kc, :], rhs=wo[:, kc, hf * 512:(hf + 1) * 512], start=(kc == 0), stop=(kc == 7)),
                     reads=[("A:oT", 0), ("A:wo", 0), ("A:wo", 1)], writes=psk(b + hf))
        post_norm_update(C, b, t, gpost, gpkey)

    nxt = do_norm(0)
    pending = None
    for blk in range(4):
        tiles = list(range(blk * 4, blk * 4 + 4))
        hT, hkeys = nxt
        for c in range(8):
            b = next_ps(C)
            for kc in range(8):
                P.pe(lambda e, kc=kc, b=b, c=c, hT=hT: e.matmul(C.PS[:, b, :], lhsT=wq[:, c, kc, :], rhs=hT[:, kc, :], start=(kc == 0), stop=(kc == 7)),
                     reads=[("A:wq", c // 4)] + hkeys, writes=psk(b))
            P.dve(lambda e, b=b, c=c: e.tensor_copy(out=qT[:, c, :], in_=C.PS[:, b, :]), reads=psk(b), writes=[("A:qT", c)])
        if pending is not None:
            out_proj(pending)
            pending = None
        for h in range(4):
            for mt in range(2):
                b = next_ps(C)
                for j in range(2):
                    P.pe(lambda e, b=b, h=h, mt=mt, j=j: e.matmul(C.PS[:, b, :], lhsT=kT[:, 2 * h + j, mt * 128:(mt + 1) * 128], rhs=qT[:, 2 * h + j, :], start=(j == 0), stop=(j == 1)),
                         reads=[("A:kT", 2 * h + j), ("A:qT", 2 * h + j)], writes=psk(b))
                P.act(lambda e, b=b, h=h, mt=mt: e.activation(out=E[:, mt, h, :], in_=C.PS[:, b, :], func=AF.Exp, scale=1.0 / 16.0),
                      reads=psk(b), writes=[("A:E", mt, h)])
        for i, t in enumerate(tiles):
            for h in range(4):
                for mt in range(2):
                    P.pe(lambda e, h=h, mt=mt, i=i: e.matmul(C.PS[:, 4 + h, 0:257], lhsT=E[:, mt, h, i * 128:(i + 1) * 128], rhs=Va[:, mt, h, :], start=(mt == 0), stop=(mt == 1)),
                         reads=[("A:E", mt, h)] + vkeys, writes=psk(4 + h))
            P.dve(lambda e: e.reciprocal(out=C.small[:, 0:4], in_=C.PS[:, 4:8, 256:257].rearrange("p h o -> p (h o)")), reads=psk(4, 4), writes=["rinv"])
            P.dve(lambda e: e.tensor_tensor(out=otok, in0=C.PS[:, 4:8, 0:256], in1=C.small[:, 0:4].unsqueeze(2).broadcast_to([128, 4, 256]), op=ALU.mult),
                  reads=psk(4, 4) + ["rinv"], writes=[("A:otok", 0)])
            if pending is not None:
                out_proj(pending)
            if i == 1 and blk < 3:
                nxt = do_norm(blk + 1)
            b = next_ps(C)
            pv = C.PS[:, b, :].bitcast(BF16)
            of = otok.rearrange("p h d -> p (h d)")
            for kc in range(8):
                P.pe(lambda e, kc=kc, pv=pv: e.transpose(out=pv[:, kc * 128:(kc + 1) * 128], in_=of[:, kc * 128:(kc + 1) * 128], identity=C.ident[:]),
                     reads=[("A:otok", 0), "ident"], writes=psk(b))
            P.act(lambda e, pv=pv: e.copy(out=oT, in_=pv.rearrange("p (k t) -> p k t", t=128)), reads=psk(b), writes=[("A:oT", 0)])
            pending = t
    out_proj(pending)


_CACHE = {}


def kernel(**inputs):
    inp = {k: np.asarray(v) for k, v in inputs.items()}
    n_phase = _CACHE.get("n_phase")
    consts = host_consts()
    lay = [layer_inputs(inp, l) for l in range(2)]
    shapes = {k: v.shape for k, v in lay[0].items()}
    nc = build_program(shapes, n_phase=n_phase)
    in_maps = []
    for b in range(8):
        m = {"x": np.ascontiguousarray(inp["x"][b]), "mem": np.ascontiguousarray(inp["mem"][b])}
        for k, v in consts.items():
            m["c_" + k] = v
        for l in range(2):
            for k, v in lay[l].items():
                m["L%d_%s" % (l, k)] = v
        in_maps.append(m)
    res = run_bass_kernel_spmd(nc, in_maps, core_ids=list(range(8)))
    return np.stack([r["out"] for r in res.results], axis=0).astype(np.float32)
```
